# Optimizing a Trainium2 kernel written in Bass

```python
import math
import jax
import jax.numpy as jnp
from jax import lax
import numpy as np

D_MODEL = 2048
BATCH = 4
SEQ = 8192
DEPTH = 2
DEC_BATCH = 8
DEC_SEQ = 32
PAST_LEN = 4096

CHUNK = 64
MIX_WIDTH = D_MODEL
N_GROUPS = 4
GROUP_WIDTH = MIX_WIDTH // N_GROUPS
HEAD_DIM = 128
N_HEADS = GROUP_WIDTH // HEAD_DIM
CONV_W = 4
IDX_HEADS = 16
IDX_DIM = 64
TOPK_MAX = 256
N_MEM = 256
Q_BLOCK = 128
ROPE_THETA = 10000.0
EPS = 1e-6
SPLIT_SIZES = (GROUP_WIDTH,) * 4 + (N_HEADS, N_HEADS) + (GROUP_WIDTH,) * 8 + (IDX_HEADS * IDX_DIM, IDX_DIM, IDX_HEADS) + (GROUP_WIDTH,) * 2
IN_COLS = 14 * GROUP_WIDTH + 2 * N_HEADS + IDX_HEADS * IDX_DIM + IDX_DIM + IDX_HEADS

kernel_name = 'hybrid_streaming_encoder_step'


def _split_points():
    return [int(p) for p in np.cumsum(SPLIT_SIZES)[:-1]]


def rmsnorm(x, g):
    xf = x.astype(jnp.float32)
    y = xf * lax.rsqrt(jnp.mean(xf * xf, axis=-1, keepdims=True) + EPS)
    return (y * g.astype(jnp.float32)).astype(x.dtype)


def head_layernorm(x, g, b):
    xf = x.astype(jnp.float32)
    mu = jnp.mean(xf, axis=-1, keepdims=True)
    xc = xf - mu
    var = jnp.mean(xc * xc, axis=-1, keepdims=True)
    return xc * lax.rsqrt(var + EPS) * g.astype(jnp.float32) + b.astype(jnp.float32)


def l2norm(x):
    xf = x.astype(jnp.float32)
    return xf * lax.rsqrt(jnp.sum(xf * xf, axis=-1, keepdims=True) + EPS)


def heads(t):
    return t.reshape(t.shape[:2] + (N_HEADS, HEAD_DIM))


def rotary(x, pos):
    half = x.shape[-1] // 2
    inv = ROPE_THETA ** (-jnp.arange(half, dtype=jnp.float32) / half)
    ang = pos.astype(jnp.float32)[:, None] * inv[None, :]
    cos = jnp.cos(ang)[None, :, None, :]
    sin = jnp.sin(ang)[None, :, None, :]
    xf = x.astype(jnp.float32)
    x1, x2 = xf[..., :half], xf[..., half:]
    return jnp.concatenate([x1 * cos - x2 * sin, x1 * sin + x2 * cos], axis=-1)


def causal_conv(x, buf, w):
    l = x.shape[1]
    xp = jnp.concatenate([buf.astype(x.dtype), x], axis=1)
    y = xp[:, 0:l] * w[0]
    for j in range(1, CONV_W):
        y = y + xp[:, j:j + l] * w[j]
    return jax.nn.silu(y), xp[:, -(CONV_W - 1):]


def _to_chunks(t, c):
    b, l, h = t.shape[:3]
    t = t.astype(jnp.float32).reshape((b, l // c, c, h) + t.shape[3:])
    return jnp.moveaxis(t, (1, 3), (0, 2))


def _from_chunks(o):
    n, b, h, c, d = o.shape
    return jnp.moveaxis(o, (0, 2), (1, 3)).reshape(b, n * c, h, d)


def gated_delta_rule(q, k, v, g, beta, s0):
    l, dk = q.shape[1], q.shape[-1]
    c = min(CHUNK, l)
    qc = _to_chunks(q, c) * dk ** -0.5
    kc = _to_chunks(k, c)
    vc = _to_chunks(v, c)
    bc = _to_chunks(beta, c)
    gcum = jnp.cumsum(_to_chunks(g, c), axis=-1)
    causal = jnp.tril(jnp.ones((c, c), dtype=bool))
    strict = jnp.tril(jnp.ones((c, c), dtype=bool), -1)
    diff = jnp.where(causal, gcum[..., :, None] - gcum[..., None, :], 0.0)
    decay = jnp.where(causal, jnp.exp(diff), 0.0)
    kb = kc * bc[..., None]
    a_mat = jnp.where(strict, jnp.einsum('nbhid,nbhjd->nbhij', kb, kc) * decay, 0.0)
    eye = jnp.eye(c, dtype=jnp.float32)
    t_mat = lax.linalg.triangular_solve(eye + a_mat, jnp.broadcast_to(eye, a_mat.shape),
                                        left_side=True, lower=True, unit_diagonal=True)
    u = jnp.einsum('nbhij,nbhje->nbhie', t_mat, vc * bc[..., None])
    w = jnp.einsum('nbhij,nbhjd->nbhid', t_mat, kb * jnp.exp(gcum)[..., None])
    qk = jnp.where(causal, jnp.einsum('nbhid,nbhjd->nbhij', qc, kc) * decay, 0.0)

    def step(s, inp):
        q_i, k_i, u_i, w_i, qk_i, g_i = inp
        v_new = u_i - jnp.einsum('bhid,bhde->bhie', w_i, s)
        o = (jnp.einsum('bhid,bhde->bhie', q_i * jnp.exp(g_i)[..., None], s)
             + jnp.einsum('bhij,bhje->bhie', qk_i, v_new))
        g_last = g_i[..., -1:]
        s = (s * jnp.exp(g_last)[..., None]
             + jnp.einsum('bhid,bhie->bhde', k_i * jnp.exp(g_last - g_i)[..., None], v_new))
        return s, o

    s, o = lax.scan(step, s0.astype(jnp.float32), (qc, kc, u, w, qk, gcum))
    return _from_chunks(o), s


def retention(q, k, v, log_gamma, s0):
    l, dk = q.shape[1], q.shape[-1]
    c = min(CHUNK, l)
    qc = _to_chunks(q, c)
    kc = _to_chunks(k, c) * dk ** -0.5
    vc = _to_chunks(v, c)
    idx = jnp.arange(c, dtype=jnp.float32)
    lg = log_gamma[:, None]
    causal = jnp.tril(jnp.ones((c, c), dtype=bool))
    rel = jnp.where(causal, idx[:, None] - idx[None, :], 0.0)
    d_mat = jnp.where(causal, jnp.exp(lg[..., None] * rel), 0.0)
    cross_decay = jnp.exp(lg * (idx + 1.0))[:, :, None]
    state_decay = jnp.exp(lg * (c - 1.0 - idx))[:, :, None]
    chunk_decay = jnp.exp(lg * c)[:, :, None]
    o_intra = jnp.einsum('nbhqk,nbhke->nbhqe', jnp.einsum('nbhqd,nbhkd->nbhqk', qc, kc) * d_mat, vc)

    def step(s, inp):
        q_i, k_i, v_i = inp
        o_cross = jnp.einsum('bhqd,bhde->bhqe', q_i, s) * cross_decay
        s = s * chunk_decay + jnp.einsum('bhkd,bhke->bhde', k_i * state_decay, v_i)
        return s, o_cross

    s, o_cross = lax.scan(step, s0.astype(jnp.float32), (qc, kc, vc))
    return _from_chunks(o_intra + o_cross), s


def dsa_attend(q, qi, wi, q_pos, k, v, ki, k_pos):
    b, l = q.shape[:2]
    qb = min(Q_BLOCK, l)
    nb = l // qb
    topk = min(TOPK_MAX, k.shape[1] // 4)
    gather = jax.vmap(lambda t, i: t[i])
    ki32 = ki.astype(jnp.float32)
    k_chunk = k_pos // CHUNK

    def blocks(t):
        return jnp.moveaxis(t.reshape((b, nb, qb) + t.shape[2:]), 1, 0)

    def one_block(args):
        q_b, qi_b, wi_b, pos_b = args
        score = jnp.einsum('bthi,bsi->bths', qi_b.astype(jnp.float32), ki32) * IDX_DIM ** -0.5
        index = jnp.einsum('bth,bths->bts', wi_b.astype(jnp.float32), jax.nn.relu(score))
        admissible = k_chunk[None, :] <= (pos_b // CHUNK)[:, None]
        index = jnp.where(admissible[None], index, -jnp.inf)
        _, sel = lax.top_k(index, topk)
        k_sel = gather(k, sel).astype(jnp.float32)
        v_sel = gather(v, sel).astype(jnp.float32)
        valid = k_chunk[sel] <= (pos_b // CHUNK)[None, :, None]
        logits = jnp.einsum('bthd,btkhd->bhtk', q_b.astype(jnp.float32), k_sel) * HEAD_DIM ** -0.5
        logits = jnp.where(valid[:, None], logits, -jnp.inf)
        p = jax.nn.softmax(logits, axis=-1)
        return jnp.einsum('bhtk,btkhd->bthd', p, v_sel)

    o = lax.map(one_block, (blocks(q), blocks(qi), blocks(wi), q_pos.reshape(nb, qb)))
    return jnp.moveaxis(o, 0, 1).reshape(q.shape)


def mem_attend(q, mk, mv):
    logits = jnp.einsum('bthd,bmhd->bhtm', q.astype(jnp.float32), mk.astype(jnp.float32)) * HEAD_DIM ** -0.5
    p = jax.nn.softmax(logits, axis=-1)
    return jnp.einsum('bhtm,bmhd->bthd', p, mv.astype(jnp.float32))


def memory_kv(mem, mem_norm_g, w_mem_kv, mem_k_norm_g):
    m = rmsnorm(mem, mem_norm_g)
    mk, mv = jnp.split(jnp.einsum('bmd,dc->bmc', m, w_mem_kv), 2, axis=-1)
    return rmsnorm(heads(mk), mem_k_norm_g), heads(mv)


def mixer_layer(x, conv_buf, s_gdn, s_ret, past_k, past_v, past_ki, mem_k, mem_v,
                norm_g, w_in, conv_w, a_log, dt_bias, gdn_norm_g, ret_norm_g, ret_norm_b,
                dsa_q_norm_g, dsa_k_norm_g, idx_k_norm_g, mem_q_norm_g, w_out):
    b, l, _ = x.shape
    offset = 0 if past_k is None else past_k.shape[1]
    pos = offset + jnp.arange(l, dtype=jnp.int32)
    h = rmsnorm(x, norm_g)
    proj = jnp.einsum('bld,dc->blc', h, w_in)
    (qa, ka, va, za, ba, aa, qb, kb, vb, zb, qc, kc, vc, zc,
     qi, ki, wi, qd, zd) = jnp.split(proj, _split_points(), axis=-1)

    qkv, conv_new = causal_conv(jnp.concatenate([qa, ka, va], axis=-1), conv_buf, conv_w)
    qa, ka, va = jnp.split(qkv, 3, axis=-1)
    beta = jax.nn.sigmoid(ba.astype(jnp.float32))
    g = -jnp.exp(a_log.astype(jnp.float32)) * jax.nn.softplus(aa.astype(jnp.float32) + dt_bias.astype(jnp.float32))
    o_a, s_gdn_new = gated_delta_rule(l2norm(heads(qa)), l2norm(heads(ka)), heads(va), g, beta, s_gdn)
    o_a = rmsnorm(o_a, gdn_norm_g) * jax.nn.silu(heads(za).astype(jnp.float32))

    log_gamma = jnp.log(1.0 - 2.0 ** (-5.0 - jnp.arange(N_HEADS, dtype=jnp.float32)))
    o_b, s_ret_new = retention(rotary(heads(qb), pos), rotary(heads(kb), pos), heads(vb), log_gamma, s_ret)
    o_b = head_layernorm(o_b, ret_norm_g, ret_norm_b) * jax.nn.silu(heads(zb).astype(jnp.float32))

    qc = rmsnorm(heads(qc), dsa_q_norm_g)
    kc = rmsnorm(heads(kc), dsa_k_norm_g)
    vc = heads(vc)
    qi = qi.reshape(b, l, IDX_HEADS, IDX_DIM)
    ki = rmsnorm(ki, idx_k_norm_g)
    wi = wi * IDX_HEADS ** -0.5
    if past_k is None:
        k_all, v_all, ki_all = kc, vc, ki
    else:
        k_all = jnp.concatenate([past_k.astype(kc.dtype), kc], axis=1)
        v_all = jnp.concatenate([past_v.astype(vc.dtype), vc], axis=1)
        ki_all = jnp.concatenate([past_ki.astype(ki.dtype), ki], axis=1)
    k_pos = jnp.arange(k_all.shape[1], dtype=jnp.int32)
    o_c = dsa_attend(qc, qi, wi, pos, k_all, v_all, ki_all, k_pos) * jax.nn.silu(heads(zc).astype(jnp.float32))

    qd = rmsnorm(heads(qd), mem_q_norm_g)
    o_d = mem_attend(qd, mem_k, mem_v) * jax.nn.silu(heads(zd).astype(jnp.float32))

    mix = jnp.concatenate([o.reshape(b, l, GROUP_WIDTH) for o in (o_a, o_b, o_c, o_d)], axis=-1).astype(x.dtype)
    y = x + jnp.einsum('blc,cd->bld', mix, w_out).astype(x.dtype)
    return y, (conv_new, s_gdn_new.astype(x.dtype), s_ret_new.astype(x.dtype), kc, vc, ki)


def setup_inputs(seed: int = 0) -> dict:
    key = jax.random.key(seed)
    ks = jax.random.split(key, 32)
    f32 = jnp.float32

    def nrm(k, shape, s=1.0):
        return jax.random.normal(k, shape, f32) * s

    def gain(k, n):
        return 1.0 + 0.02 * jax.random.normal(k, (DEPTH, n), f32)

    dt = jnp.exp(jax.random.uniform(ks[24], (DEPTH, N_HEADS), f32, math.log(1e-3), math.log(1e-1)))
    return {
        'x_prompt': nrm(ks[0], (BATCH, SEQ, D_MODEL)),
        'x_sample': nrm(ks[1], (DEC_BATCH, DEC_SEQ, D_MODEL)),
        'cache_gdn_conv': nrm(ks[2], (DEPTH, DEC_BATCH, CONV_W - 1, 3 * GROUP_WIDTH)),
        'state_gdn': nrm(ks[3], (DEPTH, DEC_BATCH, N_HEADS, HEAD_DIM, HEAD_DIM), 0.1),
        'state_ret': nrm(ks[4], (DEPTH, DEC_BATCH, N_HEADS, HEAD_DIM, HEAD_DIM), 0.1),
        'cache_dsa_k': nrm(ks[5], (DEPTH, DEC_BATCH, PAST_LEN, N_HEADS, HEAD_DIM)),
        'cache_dsa_v': nrm(ks[6], (DEPTH, DEC_BATCH, PAST_LEN, N_HEADS, HEAD_DIM)),
        'cache_idx_k': nrm(ks[7], (DEPTH, DEC_BATCH, PAST_LEN, IDX_DIM)),
        'cache_mem_k': nrm(ks[8], (DEPTH, DEC_BATCH, N_MEM, N_HEADS, HEAD_DIM)),
        'cache_mem_v': nrm(ks[9], (DEPTH, DEC_BATCH, N_MEM, N_HEADS, HEAD_DIM)),
        'mem_prompt': nrm(ks[10], (BATCH, N_MEM, D_MODEL)),
        'norm_g': gain(ks[11], D_MODEL),
        'w_in': nrm(ks[12], (DEPTH, D_MODEL, IN_COLS), D_MODEL ** -0.5),
        'gdn_conv_w': nrm(ks[13], (DEPTH, CONV_W, 3 * GROUP_WIDTH), CONV_W ** -0.5),
        'gdn_a_log': jnp.log(jax.random.uniform(ks[14], (DEPTH, N_HEADS), f32, 1.0, 16.0)),
        'gdn_dt_bias': dt + jnp.log(-jnp.expm1(-dt)),
        'gdn_norm_g': gain(ks[15], HEAD_DIM),
        'ret_norm_g': gain(ks[16], HEAD_DIM),
        'ret_norm_b': nrm(ks[17], (DEPTH, HEAD_DIM), 0.02),
        'dsa_q_norm_g': gain(ks[18], HEAD_DIM),
        'dsa_k_norm_g': gain(ks[19], HEAD_DIM),
        'idx_k_norm_g': gain(ks[20], IDX_DIM),
        'mem_norm_g': gain(ks[21], D_MODEL),
        'w_mem_kv': nrm(ks[22], (DEPTH, D_MODEL, 2 * GROUP_WIDTH), D_MODEL ** -0.5),
        'mem_q_norm_g': gain(ks[23], HEAD_DIM),
        'mem_k_norm_g': gain(ks[25], HEAD_DIM),
        'w_out': nrm(ks[26], (DEPTH, MIX_WIDTH, D_MODEL), MIX_WIDTH ** -0.5),
    }


def reference(x_prompt, x_sample, cache_gdn_conv, state_gdn, state_ret, cache_dsa_k, cache_dsa_v,
              cache_idx_k, cache_mem_k, cache_mem_v, mem_prompt, norm_g, w_in, gdn_conv_w, gdn_a_log,
              gdn_dt_bias, gdn_norm_g, ret_norm_g, ret_norm_b, dsa_q_norm_g, dsa_k_norm_g, idx_k_norm_g,
              mem_norm_g, w_mem_kv, mem_q_norm_g, mem_k_norm_g, w_out):
    b = x_prompt.shape[0]
    y_prompt, y_sample = x_prompt, x_sample
    st_p, st_s, mem_p = [], [], []
    for l in range(DEPTH):
        weights = (norm_g[l], w_in[l], gdn_conv_w[l], gdn_a_log[l], gdn_dt_bias[l], gdn_norm_g[l],
                   ret_norm_g[l], ret_norm_b[l], dsa_q_norm_g[l], dsa_k_norm_g[l], idx_k_norm_g[l],
                   mem_q_norm_g[l], w_out[l])
        mk, mv = memory_kv(mem_prompt, mem_norm_g[l], w_mem_kv[l], mem_k_norm_g[l])
        conv0 = jnp.zeros((b, CONV_W - 1, 3 * GROUP_WIDTH), x_prompt.dtype)
        s0 = jnp.zeros((b, N_HEADS, HEAD_DIM, HEAD_DIM), jnp.float32)
        y_prompt, sp = mixer_layer(y_prompt, conv0, s0, s0, None, None, None, mk, mv, *weights)
        st_p.append(sp)
        mem_p.append((mk, mv))
        y_sample, ss = mixer_layer(y_sample, cache_gdn_conv[l], state_gdn[l], state_ret[l],
                                   cache_dsa_k[l], cache_dsa_v[l], cache_idx_k[l],
                                   cache_mem_k[l], cache_mem_v[l], *weights)
        st_s.append(ss)
    new_conv_p = jnp.stack([s[0] for s in st_p])
    new_gdn_p = jnp.stack([s[1] for s in st_p])
    new_ret_p = jnp.stack([s[2] for s in st_p])
    new_dsa_k_p = jnp.stack([s[3] for s in st_p])
    new_dsa_v_p = jnp.stack([s[4] for s in st_p])
    new_idx_k_p = jnp.stack([s[5] for s in st_p])
    new_mem_k_p = jnp.stack([m[0] for m in mem_p])
    new_mem_v_p = jnp.stack([m[1] for m in mem_p])
    new_conv_s = jnp.stack([s[0] for s in st_s])
    new_gdn_s = jnp.stack([s[1] for s in st_s])
    new_ret_s = jnp.stack([s[2] for s in st_s])
    new_dsa_k_s = jnp.stack([s[3] for s in st_s])
    new_dsa_v_s = jnp.stack([s[4] for s in st_s])
    new_idx_k_s = jnp.stack([s[5] for s in st_s])
    return (y_prompt, y_sample, new_conv_p, new_gdn_p, new_ret_p, new_dsa_k_p, new_dsa_v_p, new_idx_k_p,
            new_mem_k_p, new_mem_v_p, new_conv_s, new_gdn_s, new_ret_s, new_dsa_k_s, new_dsa_v_s, new_idx_k_s)
```

```python
import math
from contextlib import ExitStack

import numpy as np
import concourse.bass as bass
import concourse.mybir as mybir
from concourse.bass_utils import run_bass_kernel_spmd

F32 = mybir.dt.float32
BF16 = mybir.dt.bfloat16
AF = mybir.ActivationFunctionType
ALU = mybir.AluOpType
AX = mybir.AxisListType

P = 128
D = 2048
KCN = 16
HD = 128
NH = 4
QA, KA, VA, ZA, BA, AA = 0, 512, 1024, 1536, 2048, 2052
QB, KB, VB, ZB = 2056, 2568, 3080, 3592
QC, KC, VC, ZC = 4104, 4616, 5128, 5640
QI, KI, WI, QD, ZD = 6152, 7176, 7240, 7256, 7768
INC = 8280
EPS = 1e-6
NEG = -3.0e38
NMEM = 256
TOPK = 256
SC = HD ** -0.5

C_ID, C_ONE, C_U, C_MNEG, C_MUP, C_RDM, C_CD, C_SD128, C_SD32, NCST = 0, 128, 256, 384, 512, 640, 1152, 1156, 1160, 1164
L_G, L_MG, L_CW, L_AL, L_DT, L_GDN, L_RG, L_RB, L_DQ, L_MQ, L_DK, L_MK, L_IK, NLP = 0, 16, 32, 80, 84, 88, 89, 90, 91, 92, 93, 221, 349, 413


class Trk:
    __slots__ = ("w", "r", "dram", "ws")

    def __init__(self, dram=False):
        self.w = None
        self.r = {}
        self.dram = dram
        self.ws = {}


class Eng:
    def __init__(self, nc, name, e, is_pe=False):
        self.name = name
        self.e = e
        self.sem = nc.alloc_semaphore("s_" + name)
        self.key = "E_" + name
        self.cnt = 0
        self.wm = {}
        self.is_pe = is_pe


class FW:
    def __init__(self, n_dma_sems=32, same_engine_sync=True):
        self.nc = bass.Bass("TRN2", target_bir_lowering=False)
        nc = self.nc
        self.pe = Eng(nc, "pe", nc.tensor, True)
        self.act = Eng(nc, "act", nc.scalar)
        self.dve = Eng(nc, "dve", nc.vector)
        self.pool = Eng(nc, "pool", nc.gpsimd)
        self.sp = Eng(nc, "sp", nc.sync)
        self.engs = [self.pe, self.act, self.dve, self.pool, self.sp]
        self.dsems = [nc.alloc_semaphore("d%d" % i) for i in range(n_dma_sems)]
        self.dcnt = [0] * n_dma_sems
        self.dpers = [True] * n_dma_sems
        self.dnext = 0
        self.dnext_sw = 0
        self.n_sw = 8
        self.same_engine_sync = same_engine_sync
        self.n_ins = 0
        self.n_wait = 0

    def _waits(self, E, reads, writes):
        need = {}

        def add(k, s, v):
            o = need.get(k)
            if o is None or o[1] < v:
                need[k] = (s, v)
        for t in reads:
            if t.dram:
                for k, (s, v) in t.ws.items():
                    add(k, s, v)
            elif t.w is not None:
                add(*t.w)
        for t in writes:
            if (not t.dram) and t.w is not None:
                add(*t.w)
            for k, (s, v) in t.r.items():
                add(k, s, v)
        for k, (s, v) in need.items():
            if k == E.key and (E.is_pe or not self.same_engine_sync):
                continue
            if E.wm.get(k, 0) >= v:
                continue
            E.e.wait_ge(s, v)
            E.wm[k] = v
            self.n_wait += 1

    @staticmethod
    def _commit(tok, reads, writes):
        k, s, v = tok
        for t in reads:
            o = t.r.get(k)
            if o is None or o[1] < v:
                t.r[k] = (s, v)
        for t in writes:
            if t.dram:
                o = t.ws.get(k)
                if o is None or o[1] < v:
                    t.ws[k] = (s, v)
            else:
                t.w = tok
                t.r = {}

    def op(self, E, fn, reads, writes):
        self._waits(E, reads, writes)
        ins = fn()
        E.cnt += 1
        ins.then_inc(E.sem, 1)
        self.n_ins += 1
        self._commit((E.key, E.sem, E.cnt), reads, writes)

    def dma(self, E, out, in_, reads, writes, persistent=False, **kw):
        self._waits(E, reads, writes)
        if E is self.pool:
            i = self.dnext_sw
            self.dnext_sw = (self.dnext_sw + 1) % self.n_sw
        else:
            i = self.n_sw + self.dnext
            self.dnext = (self.dnext + 1) % (len(self.dsems) - self.n_sw)
        s = self.dsems[i]
        k = "D%d" % i
        prev = 16 * self.dcnt[i]
        if prev > 0 and E.wm.get(k, 0) < prev:
            E.e.wait_ge(s, prev)
            E.wm[k] = prev
            self.n_wait += 1
        E.e.dma_start(out=out, in_=in_, **kw).then_inc(s, 16)
        self.dcnt[i] += 1
        self.dpers[i] = persistent
        self.n_ins += 1
        self._commit((k, s, 16 * self.dcnt[i]), reads, writes)

    def barrier(self):
        for E in self.engs:
            for i, s in enumerate(self.dsems):
                v = 16 * self.dcnt[i]
                k = "D%d" % i
                if v > 0 and (not self.dpers[i]) and E.wm.get(k, 0) < v:
                    E.e.wait_ge(s, v)
                    E.wm[k] = v
                    self.n_wait += 1
            for X in self.engs:
                if X is E or X.cnt == 0:
                    continue
                if E.wm.get(X.key, 0) < X.cnt:
                    E.e.wait_ge(X.sem, X.cnt)
                    E.wm[X.key] = X.cnt
                    self.n_wait += 1

    def finish(self):
        E = self.sp
        for i, s in enumerate(self.dsems):
            v = 16 * self.dcnt[i]
            if v > 0:
                E.e.wait_ge(s, v)
        for X in self.engs:
            if X is not E and X.cnt > 0:
                E.e.wait_ge(X.sem, X.cnt)


class Builder:
    def __init__(self, T_, TS, PAST, DEPTH, dbg_cols=0):
        self.T_, self.TS, self.PAST, self.DEPTH = T_, TS, PAST, DEPTH
        self.fw = FW()
        self.nc = self.fw.nc
        self.reg = {}
        self.uid = 0
        self.dbg_cols = dbg_cols
        self.dbg_off = 0
        self.marks = []

    def mark(self, label):
        self.marks.append((label, self.fw.pe.cnt))

    def _name(self, n):
        self.uid += 1
        return "%s_%d" % (n, self.uid)

    def sb(self, n, shape, dt=F32, es=None):
        name = self._name(n)
        if es is None:
            t = self.nc.alloc_sbuf_tensor(name, list(shape), dt)
        else:
            t = es.enter_context(self.nc.sbuf_tensor(name, list(shape), dt))
        self.reg[name] = Trk()
        return t

    def din(self, n, shape, dt=F32):
        t = self.nc.dram_tensor(n, list(shape), dt, kind="ExternalInput").ap()
        self.reg[n] = Trk(dram=True)
        return t

    def dout(self, n, shape, dt=F32):
        t = self.nc.dram_tensor(n, list(shape), dt, kind="ExternalOutput").ap()
        self.reg[n] = Trk(dram=True)
        return t

    def dscr(self, n, shape, dt=F32):
        t = self.nc.dram_tensor(n, list(shape), dt, kind="Internal").ap()
        self.reg[n] = Trk(dram=True)
        return t

    def _t(self, *aps):
        out = []
        for a in aps:
            if hasattr(a, "tensor"):
                out.append(self.reg[a.tensor.name])
        return out

    def init_banks(self):
        self.banks = []
        for i in range(8):
            name = "bank%d" % i
            t = self.nc.alloc_psum_tensor(name, [P, 512], F32)
            self.reg[name] = Trk()
            self.banks.append(t)
        self.bank_free = list(range(8))
        self.bank_rr = 0

    def bank(self):
        i = self.bank_free[self.bank_rr % len(self.bank_free)]
        self.bank_rr += 1
        return self.banks[i]

    def bank_hold(self, k):
        got = []
        for _ in range(k):
            i = self.bank_free[self.bank_rr % len(self.bank_free)]
            self.bank_free.remove(i)
            got.append(i)
        return got

    def bank_release(self, ids):
        for i in ids:
            self.bank_free.append(i)
        self.bank_free.sort()

    def mm(self, out, lhsT, rhs, start=True, stop=True):
        nc = self.nc
        self.fw.op(self.fw.pe, lambda: nc.tensor.matmul(out, lhsT=lhsT, rhs=rhs, start=start, stop=stop),
                   self._t(lhsT, rhs), self._t(out))

    def tr(self, out, in_, ident):
        nc = self.nc
        self.fw.op(self.fw.pe, lambda: nc.tensor.transpose(out, in_, ident), self._t(in_, ident), self._t(out))

    def act(self, out, in_, func, bias=None, scale=None, accum_out=None):
        nc = self.nc
        kw = {}
        if bias is not None:
            kw["bias"] = bias
        if scale is not None:
            kw["scale"] = scale
        if accum_out is not None:
            kw["accum_out"] = accum_out
        self.fw.op(self.fw.act, lambda: nc.scalar.activation(out=out, in_=in_, func=func, **kw),
                   self._t(in_, bias, scale), self._t(out, accum_out))

    def ts(self, out, in0, s1, op0, s2=None, op1=None, accum_out=None, eng=None):
        nc = self.nc
        E = eng or self.fw.dve
        kw = {}
        if op1 is not None:
            kw["op1"] = op1
        if accum_out is not None:
            kw["accum_out"] = accum_out
        self.fw.op(E, lambda: E.e.tensor_scalar(out=out, in0=in0, scalar1=s1, scalar2=s2, op0=op0, **kw),
                   self._t(in0, s1, s2), self._t(out, accum_out))

    def tt(self, out, in0, in1, op, eng=None):
        E = eng or self.fw.dve
        self.fw.op(E, lambda: E.e.tensor_tensor(out=out, in0=in0, in1=in1, op=op), self._t(in0, in1), self._t(out))

    def stt(self, out, in0, scalar, in1, op0, op1):
        nc = self.nc
        self.fw.op(self.fw.dve, lambda: nc.vector.scalar_tensor_tensor(out=out, in0=in0, scalar=scalar, in1=in1, op0=op0, op1=op1),
                   self._t(in0, scalar, in1), self._t(out))

    def red(self, out, in_, op, axis=AX.X):
        nc = self.nc
        self.fw.op(self.fw.dve, lambda: nc.vector.tensor_reduce(out=out, in_=in_, axis=axis, op=op), self._t(in_), self._t(out))

    def recip(self, out, in_):
        nc = self.nc
        self.fw.op(self.fw.dve, lambda: nc.vector.reciprocal(out=out, in_=in_), self._t(in_), self._t(out))

    def cp(self, out, in_, eng=None):
        E = eng or self.fw.dve
        if E is self.fw.act:
            return self.act(out, in_, AF.Copy)
        self.fw.op(E, lambda: E.e.tensor_copy(out=out, in_=in_), self._t(in_), self._t(out))

    def mset(self, out, val, eng=None):
        E = eng or self.fw.dve
        self.fw.op(E, lambda: E.e.memset(out, val), [], self._t(out))

    def dma(self, out, in_, q=None, **kw):
        E = q or self.fw.sp
        self.fw.dma(E, out, in_, self._t(in_), self._t(out), **kw)

    def rsqrt_(self, out, in_, mul, eps):
        self.ts(out, in_, mul, ALU.mult, eps, ALU.add)
        self.act(out, out, AF.Sqrt)
        self.recip(out, out)

    def dbg(self, ap, rows, cols):
        if not self.dbg_cols:
            return
        self.dma(self.dbgt[0:rows, self.dbg_off:self.dbg_off + cols], ap)
        self.dbg_off += cols

    def build(self):
        T_, TS, PAST, DEPTH = self.T_, self.TS, self.PAST, self.DEPTH
        nc = self.nc
        SK = PAST + TS
        self.SMAX = max(T_, SK)
        self.x_p = self.din("x_p", [T_, D])
        self.x_s = self.din("x_s", [TS, D])
        self.conv_s = self.din("conv_s", [DEPTH, 3, 1536])
        self.sg_s = self.din("sg_s", [DEPTH, NH, HD, HD])
        self.sr_s = self.din("sr_s", [DEPTH, NH, HD, HD])
        self.ck_s = self.din("ck_s", [DEPTH, PAST, 512])
        self.cv_s = self.din("cv_s", [DEPTH, PAST, 512])
        self.cik_s = self.din("cik_s", [DEPTH, PAST, 64])
        self.cmk_s = self.din("cmk_s", [DEPTH, NMEM, 512])
        self.cmv_s = self.din("cmv_s", [DEPTH, NMEM, 512])
        self.mem_p = self.din("mem_p", [NMEM, D])
        self.w_in = self.din("w_in", [DEPTH, D, INC])
        self.w_mkv = self.din("w_mkv", [DEPTH, D, 1024])
        self.w_out = self.din("w_out", [DEPTH, D, D])
        self.lp_d = self.din("lp", [DEPTH, P, NLP])
        self.cst_d = self.din("cst", [P, NCST])
        self.rope_d = self.din("rope", [T_ + TS, 128])

        self.y_p = self.dout("y_p", [T_, D])
        self.y_s = self.dout("y_s", [TS, D])
        self.o_conv = [self.dout("o_conv_p", [DEPTH, 3, 1536]), self.dout("o_conv_s", [DEPTH, 3, 1536])]
        self.o_gdn = [self.dout("o_gdn_p", [DEPTH, NH, HD, HD]), self.dout("o_gdn_s", [DEPTH, NH, HD, HD])]
        self.o_ret = [self.dout("o_ret_p", [DEPTH, NH, HD, HD]), self.dout("o_ret_s", [DEPTH, NH, HD, HD])]
        self.o_dk = [self.dout("o_dk_p", [DEPTH, T_, 512]), self.dout("o_dk_s", [DEPTH, TS, 512])]
        self.o_dv = [self.dout("o_dv_p", [DEPTH, T_, 512]), self.dout("o_dv_s", [DEPTH, TS, 512])]
        self.o_ik = [self.dout("o_ik_p", [DEPTH, T_, 64]), self.dout("o_ik_s", [DEPTH, TS, 64])]
        self.o_mk = self.dout("o_mk_p", [DEPTH, NMEM, 512])
        self.o_mv = self.dout("o_mv_p", [DEPTH, NMEM, 512])
        if self.dbg_cols:
            self.dbgt = self.dout("dbg", [P, self.dbg_cols])
        self.y1_p = self.dscr("y1_p", [T_, D])
        self.y1_s = self.dscr("y1_s", [TS, D])
        self.kts = self.dscr("kts", [HD, (self.SMAX + 255) // 256, NH, 256], BF16)
        self.vbs = self.dscr("vbs", [self.SMAX, 512], BF16)
        c0s = ([256 * i for i in range(8)] + [BA] + [QB + 256 * i for i in range(8)] + [QC + 256 * i for i in range(8)]
               + [QI + 256 * i for i in range(4)] + [KI] + [QD + 256 * i for i in range(4)])
        self.wblk = {}
        self.wq = {}
        for l in range(DEPTH):
            for nm, src_, cl in (("w_in", self.w_in, c0s), ("w_out", self.w_out, [256 * i for i in range(8)]),
                                 ("w_mkv", self.w_mkv, [256 * i for i in range(4)])):
                t = self.dscr("q_%s_%d" % (nm, l), [len(cl), P, KCN, 256], BF16)
                self.wq[(nm, l)] = (t, src_[l], {c: i for i, c in enumerate(cl)})
        self.conv_pending = []
        self.conv_trk = Trk()

        self.init_banks()
        sb = self.sb
        self.cst = sb("cst", [P, NCST])
        self.ident_b = sb("identb", [P, P], BF16)
        self.negI = sb("negI", [P, P], BF16)
        self.negI4 = sb("negI4", [P, NH, P], BF16)
        self.ones_b = sb("onesb", [P, P], BF16)
        self.lp = sb("lpt", [P, NLP])
        self.nA = sb("nA", [P, NH])
        self.xnT = sb("xnT", [P, KCN, 512], BF16)
        self.mixT = sb("mixT", [P, KCN, 512], BF16)
        self.wr = [sb("wr%d" % i, [P, KCN, 256], BF16) for i in range(3)]
        self.wr_i = 0
        self.pref = None
        self.zT = sb("zT", [P, NH, 512], BF16)
        self.S_g = sb("S_g", [P, NH, HD])
        self.Sb_g = sb("Sb_g", [P, NH, HD], BF16)
        self.S_r = sb("S_r", [P, NH, HD])
        self.Sb_r = sb("Sb_r", [P, NH, HD], BF16)
        self.hist = sb("hist", [P, 12, 3])
        self.kiT2 = sb("kiT2", [P, self.SMAX], BF16)
        self.mkT = sb("mkT", [P, NH, NMEM], BF16)
        self.mvx = sb("mvx", [P, 2, NH, HD + 1], BF16)
        self.sm = sb("sm", [P, 64])
        self.junk = sb("junk", [P, P], BF16)
        self.junk4 = sb("junk4", [P, 4, P], BF16)
        self.qb24 = sb("qb24", [P, 4, 2, P], BF16)
        self.cc = sb("cc", [P, 2])
        self.p2 = sb("p2", [P, 20])
        self.xc = [sb("xc%d" % i, [P, 256]) for i in range(4)]
        self.yc = [sb("yc%d" % i, [P, 256]) for i in range(3)]

        self.ident_f = self.cst[:, C_ID:C_ID + P]
        self.ones_f = self.cst[:, C_ONE:C_ONE + P]
        self.U_f = self.cst[:, C_U:C_U + P]
        self.Mneg = self.cst[:, C_MNEG:C_MNEG + P]
        self.Mup = self.cst[:, C_MUP:C_MUP + P]
        self.retDm = self.cst[:, C_RDM:C_RDM + 512].rearrange("p (h f) -> p h f", h=NH)

        self.dma(self.cst[:], self.cst_d[:, :])
        self.cp(self.ident_b[:], self.ident_f)
        self.cp(self.ident_b[:], self.ident_f)
        self.cp(self.ones_b[:], self.ones_f)
        self.ts(self.negI[:], self.ident_f, -30000.0, ALU.mult)
        for h in range(NH):
            self.ts(self.negI4[:, h, :], self.ident_f, -30000.0, ALU.mult)
        self.mset(self.mvx[:], 1.0)
        self.mset(self.cc[:, 0:1], EPS)
        for k in range(20):
            self.mset(self.p2[:, k:k + 1], 2.0 ** (-k))
        self.mset(self.cc[:, 1:2], 1.0)
        self.epsc = self.cc[:, 0:1]
        self.onec = self.cc[:, 1:2]
        self.mset(self.mixT[:], 0.0)

        self.convert_weights(0)
        for l in range(DEPTH):
            self.cur_l = l
            if l + 1 < DEPTH:
                self.convert_weights(l + 1, limit=0)
            self.layer_params(l)
            seq = dict(kind=0, n=P, ntiles=T_ // P, st_tiles=4, l=l,
                       xsrc=self.x_p if l == 0 else self.y1_p,
                       ydst=self.y_p if l == DEPTH - 1 else self.y1_p, pos0=0, rope0=0, past=0)
            self.run_sequence(seq)
            seq = dict(kind=1, n=TS, ntiles=1, st_tiles=1, l=l,
                       xsrc=self.x_s if l == 0 else self.y1_s,
                       ydst=self.y_s if l == DEPTH - 1 else self.y1_s, pos0=PAST, rope0=T_, past=PAST)
            self.run_sequence(seq)
        self.fw.finish()
        return nc

    def layer_params(self, l):
        self.dma(self.lp[:], self.lp_d[l, :, :])
        self.act(self.nA[:], self.lp[:, L_AL:L_AL + NH], AF.Exp)
        self.ts(self.nA[:], self.nA[:], -1.0, ALU.mult)

    def convert_weights(self, l, limit=None):
        for nm in ("w_mkv", "w_in", "w_out"):
            t, w, idx = self.wq[(nm, l)]
            width = {"w_in": INC, "w_out": D, "w_mkv": 1024}[nm]
            cs = sorted(idx.keys())
            for j, c0 in enumerate(cs):
                c1 = cs[j + 1] if j + 1 < len(cs) else width
                self.conv_pending.append((t[idx[c0], :, :, 0:c1 - c0], w[:, c0:c1].rearrange("(kc p) n -> p kc n", p=P)))
        self.flush_conv(limit)

    def flush_conv(self, limit=None):
        k = 0
        while self.conv_pending and (limit is None or k < limit):
            o, i = self.conv_pending.pop(0)
            self.fw.dma(self.fw.pool, o, i, self._t(i), self._t(o) + [self.conv_trk], persistent=True)
            k += 1

    def load_w(self, wsrc, c0, ncols):
        key = (wsrc.tensor.name, wsrc.offset, c0, ncols)
        if self.pref is not None and self.pref[0] == key:
            slot = self.pref[1]
            self.pref = None
            return slot
        slot = self.wr[self.wr_i % 3]
        self.wr_i += 1
        nm = {"w_in": "w_in", "w_out": "w_out", "w_mkv": "w_mkv"}[wsrc.tensor.name]
        t, w, idx = self.wq[(nm, self.cur_l)]
        self.dma(slot[:, :, 0:ncols], t[idx[c0], :, :, 0:ncols], persistent=True)
        return slot

    def prefetch_w(self, wsrc, c0, ncols):
        assert self.pref is None
        key = (wsrc.tensor.name, wsrc.offset, c0, ncols)
        slot = self.load_w(wsrc, c0, ncols)
        self.pref = (key, slot)

    def wblocks(self, wsrc, blocks):
        nxt = self.load_w(wsrc, blocks[0][0], blocks[0][1])
        for b in range(len(blocks)):
            cur = nxt
            if b + 1 < len(blocks):
                nxt = self.load_w(wsrc, blocks[b + 1][0], blocks[b + 1][1])
            yield b, cur

    def proj_fm(self, slot, cc0, m, ntok, out):
        for kc in range(KCN):
            self.mm(out, slot[:, kc, cc0:cc0 + m], self.xnT[:, kc, 0:ntok], start=(kc == 0), stop=(kc == KCN - 1))

    def proj_tm(self, slot, ncols, off, n, out, src=None):
        src = src if src is not None else self.xnT
        for kc in range(KCN):
            self.mm(out, src[:, kc, off:off + n], slot[:, kc, 0:ncols], start=(kc == 0), stop=(kc == KCN - 1))

    def norm_rows_to_T(self, src_rows, n, gcol, dstT, off):
        sm = self.sm
        self.dma(self.xt[:n, :], src_rows)
        self.act(self.xnb[:n, :], self.xt[:n, :], AF.Square, accum_out=sm[:n, 0:1])
        self.rsqrt_(sm[:n, 0:1], sm[:n, 0:1], 1.0 / D, EPS)
        self.ts(self.xnb[:n, :], self.xt[:n, :], sm[:n, 0:1], ALU.mult)
        for half in range(2):
            bk = self.bank()
            bkb = bk[:].bitcast(BF16).rearrange("p (a b) -> p a b", a=8)
            for j in range(8):
                kc = half * 8 + j
                self.tr(bkb[:, j, 0:n], self.xnb[:n, kc * P:(kc + 1) * P], self.ident_b[:n, :n])
            g = self.lp[:, gcol + half * 8:gcol + half * 8 + 8].unsqueeze(2).to_broadcast([P, 8, n])
            self.tt(dstT[:, half * 8:half * 8 + 8, off:off + n], bkb[:, :, 0:n], g, ALU.mult)

    def run_sequence(self, seq):
        l, n, kind = seq["l"], seq["n"], seq["kind"]
        T_, PAST = self.T_, self.PAST
        fw = self.fw
        if kind == 0:
            self.mset(self.S_g[:], 0.0)
            self.mset(self.S_r[:], 0.0)
            self.mset(self.hist[:], 0.0)
            self.memory_kv(l)
        else:
            self.dma(self.S_g[:], self.sg_s[l].rearrange("h d e -> d h e"))
            self.dma(self.S_r[:], self.sr_s[l].rearrange("h d e -> d h e"))
            with nc_noncontig(self.nc):
                for c in range(12):
                    self.dma(self.hist[:, c, :], self.conv_s[l, :, c * P:(c + 1) * P].rearrange("j p -> p j"))
            self.sample_caches(l)
        self.cp(self.Sb_g[:], self.S_g[:], eng=fw.act)
        self.cp(self.Sb_r[:], self.S_r[:], eng=fw.act)

        nst = (seq["ntiles"] + seq["st_tiles"] - 1) // seq["st_tiles"]
        for st in range(nst):
            nt = min(seq["st_tiles"], seq["ntiles"] - st * seq["st_tiles"])
            ntok = nt * n if n == P else n
            tile0 = st * seq["st_tiles"]
            self.mark("phaseX")
            fw.barrier()
            with ExitStack() as esx:
                self.xt = self.sb("xt", [P, D], F32, esx)
                self.xnb = self.sb("xnb", [P, D], BF16, esx)
                for i in range(nt):
                    r0 = (tile0 + i) * P
                    self.norm_rows_to_T(seq["xsrc"][r0:r0 + n, :], n, L_G, self.xnT, i * P)
                fw.barrier()
            ctx = dict(seq=seq, st=st, nt=nt, ntok=ntok, tile0=tile0)
            firsts = [(self.w_in[l], QA, 256), (self.w_in[l], QB, 256), (self.w_in[l], QC, 256), (self.w_in[l], QD, 256), (self.w_out[l], 0, 256)]
            for gi, grp in enumerate((self.group_a, self.group_b, self.group_c, self.group_d)):
                if self.pref is None:
                    self.prefetch_w(*firsts[gi])
                fw.barrier()
                with ExitStack() as es:
                    grp(ctx, es)
                    self.prefetch_w(*firsts[gi + 1])
                    fw.barrier()
            self.mark("outproj")
            self.out_proj(ctx)
            self.flush_conv(4)
            self.mark("end_st")
            if st + 1 < nst:
                self.prefetch_w(*firsts[0])
        if kind == 1:
            self.flush_conv(None)
        self.dma(self.o_gdn[kind][l].rearrange("h d e -> d h e"), self.S_g[:])
        self.dma(self.o_ret[kind][l].rearrange("h d e -> d h e"), self.S_r[:])
        with nc_noncontig(self.nc):
            for c in range(12):
                self.dma(self.o_conv[kind][l, :, c * P:(c + 1) * P].rearrange("j p -> p j"), self.hist[:, c, :])

    def finish_heads(self, ob, n, off, base, gain=None, bias=None):
        bk = self.bank()
        bkb = bk[:].bitcast(BF16).rearrange("p (a b) -> p a b", a=8)
        for h in range(NH):
            self.tr(bkb[:, h, 0:n], ob[:n, h, :], self.ident_b[:n, :n])
        dst = self.mixT[:, base:base + NH, off:off + n]
        z = self.zT[:, :, off:off + n]
        if gain is None:
            self.tt(dst, bkb[:, 0:NH, 0:n], z, ALU.mult)
        elif bias is None:
            self.stt(dst, bkb[:, 0:NH, 0:n], gain, z, ALU.mult, ALU.mult)
        else:
            self.act(dst, bkb[:, 0:NH, 0:n], AF.Identity, bias=bias, scale=gain)
            self.tt(dst, dst, z, ALU.mult)

    def z_blocks(self, gen_pairs, ntok):
        for (b, slot), cbase in gen_pairs:
            for cc in range(2):
                bk = self.bank()
                self.proj_fm(slot, cc * P, P, ntok, bk[:, 0:ntok])
                self.act(self.zT[:, cbase + cc, 0:ntok], bk[:, 0:ntok], AF.Silu)

    def group_a(self, ctx, es):
        self.mark("projA")
        seq, nt, ntok = ctx["seq"], ctx["nt"], ctx["ntok"]
        l, n = seq["l"], seq["n"]
        sb = self.sb
        w = self.w_in[l]
        qkvT = sb("qkvT", [P, 12, 512], BF16, es)
        xp = [sb("xp", [P, 515], F32, es) for _ in range(2)]
        acc = [sb("acc", [P, 512], F32, es) for _ in range(2)]
        ba = sb("ba", [P, 4, 8], F32, es)
        blocks = [(QA + 256 * i, 256) for i in range(6)] + [(ZA, 256), (ZA + 256, 256), (BA, 8)]
        cw = self.lp
        for b, slot in self.wblocks(w, blocks):
            if b < 6:
                for cc in range(2):
                    c = 2 * b + cc
                    bk = self.bank()
                    self.proj_fm(slot, cc * P, P, ntok, bk[:, 0:ntok])
                    x_ = xp[c % 2]
                    a_ = acc[c % 2]
                    self.act(x_[:, 3:3 + ntok], bk[:, 0:ntok], AF.Copy)
                    self.cp(x_[:, 0:3], self.hist[:, c, :])
                    self.ts(a_[:, 0:ntok], x_[:, 0:ntok], cw[:, L_CW + 4 * c:L_CW + 4 * c + 1], ALU.mult)
                    for j in range(1, 4):
                        self.stt(a_[:, 0:ntok], x_[:, j:j + ntok], cw[:, L_CW + 4 * c + j:L_CW + 4 * c + j + 1], a_[:, 0:ntok], ALU.mult, ALU.add)
                    self.cp(self.hist[:, c, :], x_[:, ntok:ntok + 3])
                    self.act(qkvT[:, c, 0:ntok], a_[:, 0:ntok], AF.Silu)
            elif b < 8:
                for cc in range(2):
                    bk = self.bank()
                    self.proj_fm(slot, cc * P, P, ntok, bk[:, 0:ntok])
                    self.act(self.zT[:, 2 * (b - 6) + cc, 0:ntok], bk[:, 0:ntok], AF.Silu)
            else:
                for i in range(nt):
                    bk = self.bank()
                    self.proj_tm(slot, 8, i * P, n, bk[:n, 0:8])
                    self.act(ba[:n, i, :], bk[:n, 0:8], AF.Copy)
        self.mark("gdn")
        NSLOT = 4
        gb = self.gdn_bufs(es, NSLOT)
        for p0 in range(0, nt, NSLOT):
            tiles = list(range(p0, min(nt, p0 + NSLOT)))
            gens = [self.gdn_pre_chain(ctx, gb, gb["slots"][i - p0], i, qkvT, ba) for i in tiles]
            live = list(gens)
            while live:
                for g_ in list(live):
                    try:
                        next(g_)
                    except StopIteration:
                        live.remove(g_)
            for i in tiles:
                self.gdn_post(ctx, gb, gb["slots"][i - p0], i)

    def gdn_bufs(self, es, nslot):
        sb = self.sb
        g = {}
        g["sq"] = sb("sq", [P, 8, P], BF16, es)
        g["tmp8"] = sb("tmp8", [P, 8, P], F32, es)
        g["knT"] = sb("knT", [P, NH, P], BF16, es)
        g["nda"] = sb("nda", [P, NH, P], F32, es)
        g["ndb"] = sb("ndb", [P, NH, P], F32, es)
        g["wT"] = sb("wT", [P, NH, P], BF16, es)
        g["vnew"] = sb("vnew", [P, NH, P], BF16, es)
        g["ob"] = sb("ob", [P, NH, P], BF16, es)
        g["slots"] = []
        for s in range(nslot):
            d = {}
            d["qnT"] = sb("qnT", [P, NH, P], BF16, es)
            d["gs"] = sb("gs", [P, 48], F32, es)
            d["Pk"] = [sb("Pk", [P, NH, P], F32, es) for _ in range(2)]
            d["Qk"] = [sb("Qk", [P, NH, P], F32, es) for _ in range(2)]
            d["QKm"] = sb("QKm", [P, NH, P], BF16, es)
            d["X"] = sb("X", [P, NH, 256], F32, es)
            d["kd"] = sb("kd", [P, NH, P], BF16, es)
            d["egl"] = sb("egl", [P, NH], F32, es)
            g["slots"].append(d)
        return g

    def gdn_pre_chain(self, ctx, gb, sl, i, qkvT, ba):
        seq = ctx["seq"]
        n = seq["n"]
        off = i * P
        L = 7 if n == P else int(math.ceil(math.log2(n)))
        sq, tmp8, knT, nda, ndb = (gb[k] for k in ("sq", "tmp8", "knT", "nda", "ndb"))
        rsq = tmp8
        dg = tmp8[:, 0:4]
        x1 = tmp8[:, 4:8]
        qnT, gs, Pk, Qk, QKm, X, kd, egl = (sl[k] for k in ("qnT", "gs", "Pk", "Qk", "QKm", "X", "kd", "egl"))
        qk3 = qkvT[:, 0:8, off:off + n]
        self.tt(sq[:, :, 0:n], qk3, qk3, ALU.mult)
        for half in range(2):
            bk = self.bank()
            b3 = bk[:].rearrange("p (h f) -> p h f", h=NH)
            self.mm(b3[:, :, 0:n], self.ones_b[:, :], sq[:, 4 * half:4 * half + 4, 0:n])
            self.act(rsq[:, 4 * half:4 * half + 4, 0:n], b3[:, :, 0:n], AF.Sqrt, bias=self.epsc[:, 0:1])
        self.recip(rsq[:, :, 0:n], rsq[:, :, 0:n])
        self.tt(qnT[:, :, 0:n], qkvT[:, 0:4, off:off + n], rsq[:, 0:4, 0:n], ALU.mult)
        self.tt(knT[:, :, 0:n], qkvT[:, 4:8, off:off + n], rsq[:, 4:8, 0:n], ALU.mult)
        self.tt(gs[:n, 0:4], ba[:n, i, 4:8], self.lp[:n, L_DT:L_DT + 4], ALU.add)
        self.act(gs[:n, 4:8], gs[:n, 0:4], AF.Abs)
        self.act(gs[:n, 4:8], gs[:n, 4:8], AF.Exp, scale=-1.0)
        self.act(gs[:n, 8:12], gs[:n, 4:8], AF.Ln, bias=self.onec[:n, 0:1])
        self.stt(gs[:n, 8:12], gs[:n, 0:4], 0.0, gs[:n, 8:12], ALU.max, ALU.add)
        self.tt(gs[:n, 12:16], gs[:n, 8:12], self.nA[:n, :], ALU.mult)
        self.act(gs[:n, 16:20], ba[:n, i, 0:4], AF.Sigmoid)
        bk = self.bank()
        self.mm(bk[:n, 0:4], self.U_f[:n, :n], gs[:n, 12:16])
        self.cp(gs[:n, 20:24], bk[:n, 0:4])
        self.act(gs[:n, 24:28], gs[:n, 20:24], AF.Exp)
        for h in range(NH):
            self.ts(dg[:n, h, 0:n], self.ident_f[:n, :n], gs[:n, 20 + h:21 + h], ALU.mult)
        rb = self.bank()
        rb3 = rb[:].rearrange("p (h f) -> p h f", h=NH)
        self.mm(rb3[:, :, 0:n], self.ones_f[:n, :], dg[:n, :, 0:n])
        for h in range(NH):
            self.ts(x1[:n, h, 0:n], rb3[:n, h, 0:n], gs[:n, 20 + h:21 + h], ALU.subtract)
        self.ts(nda[:n, :, 0:n], x1[:n, :, 0:n], 0.0, ALU.max)
        self.tt(ndb[:n, :, 0:n], nda[:n, :, 0:n], x1[:n, :, 0:n], ALU.subtract)
        self.act(nda[:n, :, 0:n], nda[:n, :, 0:n], AF.Exp, scale=-1.0)
        self.act(ndb[:n, :, 0:n], ndb[:n, :, 0:n], AF.Exp, scale=-1.0)
        self.tt(nda[:n, :, 0:n], nda[:n, :, 0:n], self.Mneg[:n, 0:n].unsqueeze(1).to_broadcast([n, NH, n]), ALU.mult)
        self.tt(ndb[:n, :, 0:n], ndb[:n, :, 0:n], self.Mup[:n, 0:n].unsqueeze(1).to_broadcast([n, NH, n]), ALU.mult)
        self.act(egl[:, :], rb3[:, :, n - 1], AF.Exp)
        self.tt(gs[:n, 32:36], rb3[:n, :, n - 1], gs[:n, 20:24], ALU.subtract)
        self.act(gs[:n, 32:36], gs[:n, 32:36], AF.Exp)
        self.tt(gs[:n, 28:32], gs[:n, 16:20], gs[:n, 24:28], ALU.mult)
        self.ts(gs[:n, 36:40], gs[:n, 24:28], SC, ALU.mult)
        bk = self.bank()
        b3 = bk[:].rearrange("p (h f) -> p h f", h=NH)
        for h in range(NH):
            self.mm(b3[:n, h, 0:n], knT[:, h, 0:n], knT[:, h, 0:n])
        for h in range(NH):
            self.stt(Pk[0][:n, h, 0:n], b3[:n, h, 0:n], gs[:n, 16 + h:17 + h], nda[:n, h, 0:n], ALU.mult, ALU.mult)
        bk = self.bank()
        bb = bk[:].rearrange("p (a b) -> p a b", a=NH)
        for h in range(NH):
            self.tr(bb[:n, h, 0:n], Pk[0][:n, h, 0:n], self.ident_f[:n, :n])
        self.cp(Qk[0][:n, :, 0:n], bb[:n, 0:NH, 0:n], eng=self.fw.act)
        bk = self.bank()
        b3 = bk[:].rearrange("p (h f) -> p h f", h=NH)
        for h in range(NH):
            self.mm(b3[:n, h, 0:n], knT[:, h, 0:n], qnT[:, h, 0:n])
        self.tt(QKm[:n, :, 0:n], b3[:n, :, 0:n], ndb[:n, :, 0:n], ALU.mult)
        bk = self.bank()
        kb_ = bk[:].bitcast(BF16).rearrange("p (a b) -> p a b", a=8)
        for h in range(NH):
            self.tr(kb_[:n, h, :], knT[:, h, 0:n], self.ident_b[:, :])
        for h in range(NH):
            self.tr(kb_[:n, 4 + h, :], qkvT[:, 8 + h, off:off + n], self.ident_b[:, :])
        self.tt(X[:n, :, 128:256], kb_[:n, 0:4, :], gs[:n, 28:32].unsqueeze(2).to_broadcast([n, NH, P]), ALU.mult)
        self.tt(kd[:n, :, :], kb_[:n, 0:4, :], gs[:n, 32:36].unsqueeze(2).to_broadcast([n, NH, P]), ALU.mult)
        self.tt(X[:n, :, 0:128], kb_[:n, 4:8, :], gs[:n, 16:20].unsqueeze(2).to_broadcast([n, NH, P]), ALU.mult)
        yield
        for k in range(L):
            Pc, Qc = Pk[k % 2], Qk[k % 2]
            Pn, Qn = Pk[(k + 1) % 2], Qk[(k + 1) % 2]
            if k < L - 1:
                pb = self.bank()
                qb = self.bank()
                p3 = pb[:].rearrange("p (h f) -> p h f", h=NH)
                q3 = qb[:].rearrange("p (h f) -> p h f", h=NH)
                for h in range(NH):
                    self.mm(p3[:n, h, 0:n], Qc[:n, h, 0:n], Pc[:n, h, 0:n])
                for h in range(NH):
                    self.mm(q3[:n, h, 0:n], Pc[:n, h, 0:n], Qc[:n, h, 0:n])
            y0 = self.bank()
            y1 = self.bank()
            ys = [y0[:].rearrange("p (h f) -> p h f", h=2), y1[:].rearrange("p (h f) -> p h f", h=2)]
            for h in range(NH):
                self.mm(ys[h // 2][:n, h % 2, :], Qc[:n, h, 0:n], X[:n, h, :])
            if k < L - 1:
                self.cp(Pn[:n, :, 0:n], p3[:n, :, 0:n], eng=self.fw.act)
                self.cp(Qn[:n, :, 0:n], q3[:n, :, 0:n], eng=self.fw.act)
            for hh in range(2):
                self.tt(X[:n, 2 * hh:2 * hh + 2, :], X[:n, 2 * hh:2 * hh + 2, :], ys[hh][:n, :, :], ALU.add)
            yield

    def gdn_post(self, ctx, gb, sl, i):
        seq = ctx["seq"]
        n = seq["n"]
        off = i * P
        tmp8, wT, vnew, ob = (gb[k] for k in ("tmp8", "wT", "vnew", "ob"))
        o1s = tmp8[:, 0:4]
        o = tmp8[:, 4:8]
        qnT, gs, QKm, X, kd, egl = (sl[k] for k in ("qnT", "gs", "QKm", "X", "kd", "egl"))
        bk = self.bank()
        bb = bk[:].rearrange("p (a b) -> p a b", a=NH)
        for h in range(NH):
            self.tr(bb[:, h, 0:n], X[:n, h, 128:256], self.ident_f[:n, :n])
        self.cp(wT[:, :, 0:n], bb[:, 0:NH, 0:n], eng=self.fw.act)
        bk = self.bank()
        b3 = bk[:].rearrange("p (h f) -> p h f", h=NH)
        for h in range(NH):
            self.mm(b3[:n, h, :], wT[:, h, 0:n], self.Sb_g[:, h, :])
        self.tt(vnew[:n, :, :], X[:n, :, 0:128], b3[:n, :, :], ALU.subtract)
        o1 = self.bank()
        o13 = o1[:].rearrange("p (h f) -> p h f", h=NH)
        for h in range(NH):
            self.mm(o13[:n, h, :], qnT[:, h, 0:n], self.Sb_g[:, h, :])
        o2 = self.bank()
        o23 = o2[:].rearrange("p (h f) -> p h f", h=NH)
        for h in range(NH):
            self.mm(o23[:n, h, :], QKm[:n, h, 0:n], vnew[:n, h, :])
        sn = self.bank()
        sn3 = sn[:].rearrange("p (h f) -> p h f", h=NH)
        for h in range(NH):
            self.mm(sn3[:, h, :], kd[:n, h, :], vnew[:n, h, :])
        for h in range(NH):
            self.stt(self.S_g[:, h, :], self.S_g[:, h, :], egl[:, h:h + 1], sn3[:, h, :], ALU.mult, ALU.add)
        self.cp(self.Sb_g[:], self.S_g[:], eng=self.fw.act)
        for h in range(NH):
            self.act(o1s[:n, h, :], o13[:n, h, :], AF.Copy, scale=gs[:n, 36 + h:37 + h])
        self.tt(o[:n, :, :], o1s[:n, :, :], o23[:n, :, :], ALU.add)
        for h in range(NH):
            self.act(o1s[:n, h, :], o[:n, h, :], AF.Square, accum_out=gs[:n, 40 + h:41 + h])
        self.rsqrt_(gs[:n, 44:48], gs[:n, 40:44], 1.0 / HD, EPS)
        self.tt(ob[:n, :, :], o[:n, :, :], gs[:n, 44:48].unsqueeze(2).to_broadcast([n, NH, P]), ALU.mult)
        self.finish_heads(ob, n, off, 0, gain=self.lp[:, L_GDN:L_GDN + 1])

    def interleave(self, gens):
        live = list(gens)
        while live:
            for g_ in list(live):
                try:
                    next(g_)
                except StopIteration:
                    live.remove(g_)

    def head_rms_T_gen(self, slot, i, n, gain, dst):
        gs = self.sm[:, 16 * i:16 * i + 16]
        junk = self.junk4[:, i, :]
        qb2 = self.qb24[:, i]
        bk = self.bank()
        self.proj_tm(slot, 256, i * P, n, bk[:n, 0:256])
        yield
        b3 = bk[:n, 0:256].rearrange("p (h e) -> p h e", h=2)
        for hh in range(2):
            self.act(junk[:n, 0:P], b3[:, hh, :], AF.Square, accum_out=gs[:n, hh:hh + 1])
        yield
        self.ts(gs[:n, 4:6], gs[:n, 0:2], 1.0 / HD, ALU.mult, EPS, ALU.add)
        yield
        self.act(gs[:n, 4:6], gs[:n, 4:6], AF.Sqrt)
        yield
        self.recip(gs[:n, 4:6], gs[:n, 4:6])
        yield
        self.tt(qb2[:n, :, :], b3, gs[:n, 4:6].unsqueeze(2).to_broadcast([n, 2, P]), ALU.mult)
        yield
        tb = self.bank()
        tbb = tb[:].bitcast(BF16).rearrange("p (a b) -> p a b", a=8)
        for hh in range(2):
            self.tr(tbb[:, hh, 0:n], qb2[:n, hh, :], self.ident_b[:n, :n])
        yield
        self.act(dst, tbb[:, 0:2, 0:n], AF.Identity, scale=gain)

    def group_b(self, ctx, es):
        self.mark("projB")
        seq, nt, ntok, tile0 = ctx["seq"], ctx["nt"], ctx["ntok"], ctx["tile0"]
        l, n = seq["l"], seq["n"]
        sb = self.sb
        w = self.w_in[l]
        qrsT = sb("qrsT", [P, 4, NH, P], BF16, es)
        krT = sb("krT", [P, 4, NH, P], BF16, es)
        krs = sb("krs", [P, 4, 512], BF16, es)
        vb = sb("vb", [P, 4, 512], BF16, es)
        stg = [sb("stg", [P, 256], F32, es) for _ in range(4)]
        rt = [sb("rt", [P, 4, 2, 64], F32, es) for _ in range(4)]
        rr = [sb("rr", [P, 2, P], F32, es) for _ in range(4)]
        rrb = [sb("rrb", [P, 2, P], BF16, es) for _ in range(4)]
        ropes = sb("ropes", [P, 4, P], F32, es)
        cd = self.cst[:, C_CD:C_CD + NH]
        sd = self.cst[:, C_SD128:C_SD128 + NH] if n == P else self.cst[:, C_SD32:C_SD32 + NH]
        for i in range(nt):
            r0 = seq["rope0"] + (tile0 + i) * P
            self.dma(ropes[:n, i, :], self.rope_d[r0:r0 + n, :])
        blocks = [(QB + 256 * i, 256) for i in range(8)]
        k_ = 0
        for b, slot in self.wblocks(w, blocks):
            if b < 6:
                h0 = 2 * (b % 2)

                def bunit(i, slot=slot, h0=h0, b=b):
                    bk = self.bank()
                    self.proj_tm(slot, 256, i * P, n, bk[:n, 0:256])
                    yield
                    if b >= 4:
                        self.act(vb[:n, i, h0 * P:(h0 + 2) * P], bk[:n, 0:256], AF.Copy)
                        return
                    s = stg[i]
                    rt_, rr_, rrb_ = rt[i], rr[i], rrb[i]
                    self.act(s[:n, :], bk[:n, 0:256], AF.Copy)
                    yield
                    s4 = s[:n, :].rearrange("p (h t d) -> p h t d", h=2, t=2)
                    x1, x2 = s4[:, :, 0, :], s4[:, :, 1, :]
                    cosb = ropes[:n, i, 0:64].unsqueeze(1).to_broadcast([n, 2, 64])
                    sinb = ropes[:n, i, 64:128].unsqueeze(1).to_broadcast([n, 2, 64])
                    self.tt(rt_[:n, 0], x1, cosb, ALU.mult)
                    yield
                    self.tt(rt_[:n, 1], x2, sinb, ALU.mult)
                    yield
                    self.tt(rt_[:n, 2], x1, sinb, ALU.mult)
                    yield
                    self.tt(rt_[:n, 3], x2, cosb, ALU.mult)
                    yield
                    rr4 = rr_[:n, :, :].rearrange("p h (t d) -> p h t d", t=2)
                    self.tt(rr4[:, :, 0, :], rt_[:n, 0], rt_[:n, 1], ALU.subtract)
                    yield
                    self.tt(rr4[:, :, 1, :], rt_[:n, 2], rt_[:n, 3], ALU.add)
                    yield
                    if b < 2:
                        self.tt(rrb_[:n, :, :], rr_[:n, :, :], cd[:n, h0:h0 + 2].unsqueeze(2).to_broadcast([n, 2, P]), ALU.mult)
                        dstT = qrsT
                    else:
                        self.cp(rrb_[:n, :, :], rr_[:n, :, :])
                        self.tt(krs[:n, i, h0 * P:(h0 + 2) * P].rearrange("p (h e) -> p h e", h=2), rr_[:n, :, :],
                                sd[:n, h0:h0 + 2].unsqueeze(2).to_broadcast([n, 2, P]), ALU.mult)
                        dstT = krT
                    yield
                    tb = self.bank()
                    tbb = tb[:].bitcast(BF16).rearrange("p (a b) -> p a b", a=8)
                    for hh in range(2):
                        self.tr(tbb[:, hh, 0:n], rrb_[:n, hh, :], self.ident_b[:n, :n])
                    yield
                    self.cp(dstT[:, i, h0:h0 + 2, 0:n], tbb[:, 0:2, 0:n], eng=self.fw.act)

                self.interleave([bunit(i) for i in range(nt)])
            else:
                for cc in range(2):
                    bk = self.bank()
                    self.proj_fm(slot, cc * P, P, ntok, bk[:, 0:ntok])
                    self.act(self.zT[:, 2 * (b - 6) + cc, 0:ntok], bk[:, 0:ntok], AF.Silu)
        self.mark("ret")
        MT = sb("MT", [P, NH, P], BF16, es)
        osb = sb("osb", [P, NH, P], F32, es)
        sqt = sb("sqt", [P, NH, P], F32, es)
        ob = sb("obr", [P, NH, P], BF16, es)
        gs = self.sm
        for i in range(nt):
            off = i * P
            sc = self.bank()
            sc3 = sc[:].rearrange("p (h f) -> p h f", h=NH)
            for h in range(NH):
                self.mm(sc3[:n, h, 0:n], krT[:, i, h, 0:n], qrsT[:, i, h, 0:n])
            self.tt(MT[:n, :, 0:n], sc3[:n, :, 0:n], self.retDm[:n, :, 0:n], ALU.mult)
            ob_ = self.bank()
            o3 = ob_[:].rearrange("p (h f) -> p h f", h=NH)
            for h in range(NH):
                self.mm(o3[:n, h, :], MT[:n, h, 0:n], vb[:n, i, h * P:(h + 1) * P], start=True, stop=False)
                self.mm(o3[:n, h, :], qrsT[:, i, h, 0:n], self.Sb_r[:, h, :], start=False, stop=True)
            self.act(osb[:n, :, :], o3[:n, :, :], AF.Copy)
            self.red(gs[:n, 0:4], osb[:n, :, :], ALU.add)
            self.tt(sqt[:n, :, :], osb[:n, :, :], osb[:n, :, :], ALU.mult)
            self.red(gs[:n, 4:8], sqt[:n, :, :], ALU.add)
            self.ts(gs[:n, 8:12], gs[:n, 0:4], 1.0 / HD, ALU.mult)
            self.tt(gs[:n, 12:16], gs[:n, 8:12], gs[:n, 8:12], ALU.mult)
            self.stt(gs[:n, 16:20], gs[:n, 4:8], 1.0 / HD, gs[:n, 12:16], ALU.mult, ALU.subtract)
            self.rsqrt_(gs[:n, 20:24], gs[:n, 16:20], 1.0, EPS)
            self.tt(sqt[:n, :, :], osb[:n, :, :], gs[:n, 8:12].unsqueeze(2).to_broadcast([n, NH, P]), ALU.subtract)
            self.tt(ob[:n, :, :], sqt[:n, :, :], gs[:n, 20:24].unsqueeze(2).to_broadcast([n, NH, P]), ALU.mult)
            self.finish_heads(ob, n, off, 4, gain=self.lp[:, L_RG:L_RG + 1], bias=self.lp[:, L_RB:L_RB + 1])
            sn = self.bank()
            sn3 = sn[:].rearrange("p (h f) -> p h f", h=NH)
            for h in range(NH):
                self.mm(sn3[:, h, :], krs[:n, i, h * P:(h + 1) * P], vb[:n, i, h * P:(h + 1) * P])
            for h in range(NH):
                gam = (1.0 - 2.0 ** (-5.0 - h)) ** n
                self.stt(self.S_r[:, h, :], self.S_r[:, h, :], float(gam), sn3[:, h, :], ALU.mult, ALU.add)
            self.cp(self.Sb_r[:], self.S_r[:], eng=self.fw.act)

    def group_d(self, ctx, es):
        self.mark("projD")
        seq, nt, ntok = ctx["seq"], ctx["nt"], ctx["ntok"]
        l, n = seq["l"], seq["n"]
        sb = self.sb
        w = self.w_in[l]
        QdT = sb("QdT", [P, 4, NH, P], BF16, es)
        E2 = sb("E2", [P, 2, NH, P], BF16, es)
        ob = sb("obd", [P, NH, P], BF16, es)
        gs = self.sm
        blocks = [(QD, 256), (QD + 256, 256), (ZD, 256), (ZD + 256, 256)]
        for b, slot in self.wblocks(w, blocks):
            if b < 2:
                self.interleave([self.head_rms_T_gen(slot, i, n, self.lp[:, L_MQ:L_MQ + 1], QdT[:, i, 2 * b:2 * b + 2, 0:n])
                                 for i in range(nt)])
            else:
                for cc in range(2):
                    bk = self.bank()
                    self.proj_fm(slot, cc * P, P, ntok, bk[:, 0:ntok])
                    self.act(self.zT[:, 2 * (b - 2) + cc, 0:ntok], bk[:, 0:ntok], AF.Silu)
        self.mark("mem")
        for i in range(nt):
            off = i * P
            for mb in range(2):
                lg = self.bank()
                lg3 = lg[:].rearrange("p (h f) -> p h f", h=NH)
                for h in range(NH):
                    self.mm(lg3[:, h, 0:n], self.mkT[:, h, mb * P:(mb + 1) * P], QdT[:, i, h, 0:n])
                self.act(E2[:, mb, :, 0:n], lg3[:, :, 0:n], AF.Exp, scale=SC)
            for h in range(NH):
                acc = self.bank()
                for mb in range(2):
                    self.mm(acc[:n, 0:HD + 1], E2[:, mb, h, 0:n], self.mvx[:, mb, h, :], start=(mb == 0), stop=(mb == 1))
                self.recip(gs[:n, h:h + 1], acc[:n, HD:HD + 1])
                self.ts(ob[:n, h, :], acc[:n, 0:HD], gs[:n, h:h + 1], ALU.mult)
            self.finish_heads(ob, n, off, 12)

    def memory_kv(self, l):
        self.mark("memkv")
        sb = self.sb
        gs = self.sm
        self.fw.barrier()
        with ExitStack() as es:
            kf = [sb("mkf", [P, 2, P], F32, es) for _ in range(2)]
            kb16 = sb("mkb", [P, 2, P], BF16, es)
            self.xt = sb("xt", [P, D], F32, es)
            self.xnb = sb("xnb", [P, D], BF16, es)
            for mt in range(2):
                self.norm_rows_to_T(self.mem_p[mt * P:(mt + 1) * P, :], P, L_MG, self.xnT, mt * P)
            blocks = [(256 * b, 256) for b in range(4)]
            k_ = 0
            for b, slot in self.wblocks(self.w_mkv[l], blocks):
                for mt in range(2):
                    bk = self.bank()
                    self.proj_tm(slot, 256, mt * P, P, bk[:, 0:256])
                    b3 = bk[:, 0:256].rearrange("p (h e) -> p h e", h=2)
                    f = kf[k_ % 2]
                    k_ += 1
                    if b < 2:
                        for hh in range(2):
                            self.act(self.junk[:, 0:P], b3[:, hh, :], AF.Square, accum_out=gs[:, hh:hh + 1])
                        self.rsqrt_(gs[:, 4:6], gs[:, 0:2], 1.0 / HD, EPS)
                        self.tt(f[:, :, :], b3, gs[:, 4:6].unsqueeze(2).to_broadcast([P, 2, P]), ALU.mult)
                        self.tt(f[:, :, :], f[:, :, :], self.lp[:, L_MK:L_MK + P].unsqueeze(1).to_broadcast([P, 2, P]), ALU.mult)
                        self.dma(self.o_mk[l, mt * P:(mt + 1) * P, b * 256:(b + 1) * 256].rearrange("p (h e) -> p h e", h=2), f[:, :, :])
                        self.cp(kb16[:, :, :], f[:, :, :])
                        tb = self.bank()
                        tbb = tb[:].bitcast(BF16).rearrange("p (a b) -> p a b", a=8)
                        for hh in range(2):
                            self.tr(tbb[:, hh, :], kb16[:, hh, :], self.ident_b[:, :])
                        self.cp(self.mkT[:, 2 * b:2 * b + 2, mt * P:(mt + 1) * P], tbb[:, 0:2, :], eng=self.fw.act)
                    else:
                        self.act(f[:, :, :], b3, AF.Copy)
                        self.dma(self.o_mv[l, mt * P:(mt + 1) * P, (b - 2) * 256:(b - 1) * 256].rearrange("p (h e) -> p h e", h=2), f[:, :, :])
                        self.cp(self.mvx[:, mt, 2 * (b - 2):2 * (b - 2) + 2, 0:HD], f[:, :, :])
            self.fw.barrier()

    def sample_caches(self, l):
        self.mark("scache")
        sb = self.sb
        PAST = self.PAST
        self.fw.barrier()
        with ExitStack() as es:
            stg = [sb("cstg", [P, 512], F32, es) for _ in range(2)]
            sb16 = [sb("csb", [P, 512], BF16, es) for _ in range(2)]
            ktt = [sb("cktt", [P, NH, P], BF16, es) for _ in range(2)]
            kis = [sb("ckis", [P, 64], F32, es) for _ in range(2)]
            ki2 = [sb("cki2", [P, P], BF16, es) for _ in range(2)]
            for mt in range(2):
                s, s16 = stg[mt % 2], sb16[mt % 2]
                self.dma(s[:, :], self.cmk_s[l, mt * P:(mt + 1) * P, :])
                self.cp(s16[:, :], s[:, :])
                tb = self.bank()
                tbb = tb[:].bitcast(BF16).rearrange("p (a b) -> p a b", a=8)
                for h in range(NH):
                    self.tr(tbb[:, h, :], s16[:, h * P:(h + 1) * P], self.ident_b[:, :])
                self.cp(self.mkT[:, :, mt * P:(mt + 1) * P], tbb[:, 0:NH, :], eng=self.fw.act)
                self.dma(self.mvx[:, mt, :, 0:HD], self.cmv_s[l, mt * P:(mt + 1) * P, :].rearrange("p (h e) -> p h e", h=NH), q=self.fw.pool)
            for r0 in range(0, PAST, 1024):
                r1 = min(PAST, r0 + 1024)
                self.dma(self.vbs[r0:r1, :], self.cv_s[l, r0:r1, :], q=self.fw.pool, max_dma_last_dim=2048)
            for kb in range(PAST // P):
                s, s16, kt = stg[kb % 2], sb16[kb % 2], ktt[kb % 2]
                self.dma(s[:, :], self.ck_s[l, kb * P:(kb + 1) * P, :])
                self.cp(s16[:, :], s[:, :])
                tb = self.bank()
                tbb = tb[:].bitcast(BF16).rearrange("p (a b) -> p a b", a=8)
                for h in range(NH):
                    self.tr(tbb[:, h, :], s16[:, h * P:(h + 1) * P], self.ident_b[:, :])
                self.cp(kt[:, :, :], tbb[:, 0:NH, :], eng=self.fw.act)
                self.dma(self.kts[:, kb // 2, :, (kb % 2) * P:(kb % 2) * P + P], kt[:, :, :])
                ks, k2 = kis[kb % 2], ki2[kb % 2]
                self.dma(ks[:, :], self.cik_s[l, kb * P:(kb + 1) * P, :])
                self.cp(k2[:, 0:64], ks[:, :])
                self.cp(k2[:, 64:128], ks[:, :])
                tb = self.bank()
                tbb = tb[:].bitcast(BF16)
                self.tr(tbb[:, 0:P], k2[:, :], self.ident_b[:, :])
                self.cp(self.kiT2[:, kb * P:(kb + 1) * P], tbb[:, 0:P], eng=self.fw.act)
            self.fw.barrier()

    def group_c(self, ctx, es):
        self.mark("projC")
        seq, nt, ntok, tile0 = ctx["seq"], ctx["nt"], ctx["ntok"], ctx["tile0"]
        l, n, kind, past = seq["l"], seq["n"], seq["kind"], seq["past"]
        sb = self.sb
        w = self.w_in[l]
        gs = self.sm
        QcT = sb("QcT", [P, 4, NH, P], BF16, es)
        qiZ = sb("qiZ", [P, 16, 512], BF16, es)
        qz4 = qiZ[:].rearrange("p (c two) t -> p c two t", two=2)
        self.mset(qz4[64:128, :, 0, :], 0.0)
        self.mset(qz4[0:64, :, 1, :], 0.0)
        absw = sb("absw", [P, 4, 16], F32, es)
        sgn = sb("sgn", [P, 4, 16], F32, es)
        kf = [sb("kf", [P, 2, P], F32, es) for _ in range(4)]
        kb16 = [sb("kb16", [P, 2, P], BF16, es) for _ in range(4)]
        ktt = [sb("ktt", [P, 2, P], BF16, es) for _ in range(4)]
        vf = [sb("vf", [P, 256], F32, es) for _ in range(2)]
        vb16 = [sb("vb16", [P, 256], BF16, es) for _ in range(2)]
        kif = [sb("kif", [P, 64], F32, es) for _ in range(2)]
        ki2 = sb("ki2", [P, P], BF16, es)
        index = sb("index", [P, self.SMAX], F32, es)
        cjd = sb("cjd", [P, int(0.42 * self.SMAX) + 8], mybir.dt.uint8, es)
        cja = sb("cja", [P, int(0.58 * self.SMAX) + 16], mybir.dt.uint8, es)
        dsg = sb("dsg", [P, 16, P], BF16, es)
        R = [sb("R", [P, 512], BF16, es) for _ in range(6)]
        Kx = [sb("Kx", [P, NH, 256], BF16, es) for _ in range(2)]
        Vx = [sb("Vx", [P, 2, NH, HD + 1], BF16, es) for _ in range(2)]
        PT = [sb("PT", [P, NH, P], BF16, es) for _ in range(3)]
        selb = [sb("selb", [P, P], BF16, es) for _ in range(3)]
        ob = sb("obc", [P, NH, P], BF16, es)
        bs = sb("bs", [P, 16], F32, es)
        bw = sb("bw", [P, 20], F32, es)
        bsa = sb("bsa", [P, 2], F32, es)
        bsb = sb("bsb", [P, 2], F32, es)
        bw2 = sb("bw2", [P, 20], F32, es)
        for v_ in Vx:
            self.mset(v_[:], 1.0)
        blocks = ([(QC + 256 * i, 256) for i in range(8)] + [(QI + 256 * i, 256) for i in range(4)] + [(KI, 80)])
        k_ = 0
        for b, slot in self.wblocks(w, blocks):
            if b < 2:
                self.interleave([self.head_rms_T_gen(slot, i, n, self.lp[:, L_DQ:L_DQ + 1], QcT[:, i, 2 * b:2 * b + 2, 0:n])
                                 for i in range(nt)])
            elif b < 4:
                h0 = 2 * (b - 2)

                def kunit(i, slot=slot, h0=h0):
                    g4 = self.sm[:, 16 * i:16 * i + 16]
                    junk = self.junk4[:, i, :]
                    bk = self.bank()
                    self.proj_tm(slot, 256, i * P, n, bk[:n, 0:256])
                    yield
                    b3 = bk[:n, 0:256].rearrange("p (h e) -> p h e", h=2)
                    f, kt, k16 = kf[i], ktt[i], kb16[i]
                    for hh in range(2):
                        self.act(junk[:n, 0:P], b3[:, hh, :], AF.Square, accum_out=g4[:n, hh:hh + 1])
                    yield
                    self.ts(g4[:n, 4:6], g4[:n, 0:2], 1.0 / HD, ALU.mult, EPS, ALU.add)
                    yield
                    self.act(g4[:n, 4:6], g4[:n, 4:6], AF.Sqrt)
                    yield
                    self.recip(g4[:n, 4:6], g4[:n, 4:6])
                    yield
                    self.tt(f[:n, :, :], b3, g4[:n, 4:6].unsqueeze(2).to_broadcast([n, 2, P]), ALU.mult)
                    yield
                    self.tt(f[:n, :, :], f[:n, :, :], self.lp[:n, L_DK:L_DK + P].unsqueeze(1).to_broadcast([n, 2, P]), ALU.mult)
                    yield
                    r0 = (tile0 + i) * P
                    self.dma(self.o_dk[kind][l, r0:r0 + n, h0 * P:(h0 + 2) * P].rearrange("p (h e) -> p h e", h=2), f[:n, :, :])
                    self.cp(k16[:n, :, :], f[:n, :, :])
                    yield
                    tb = self.bank()
                    tbb = tb[:].bitcast(BF16).rearrange("p (a b) -> p a b", a=8)
                    for hh in range(2):
                        self.tr(tbb[:, hh, 0:n], k16[:n, hh, :], self.ident_b[:n, :n])
                    yield
                    self.cp(kt[:, :, 0:n], tbb[:, 0:2, 0:n], eng=self.fw.act)
                    yield
                    kpos = past + r0
                    self.dma(self.kts[:, kpos // 256, h0:h0 + 2, kpos % 256:kpos % 256 + n], kt[:, :, 0:n])

                self.interleave([kunit(i) for i in range(nt)])
            elif b < 6:
                h0 = 2 * (b - 4)
                for i in range(nt):
                    bk = self.bank()
                    self.proj_tm(slot, 256, i * P, n, bk[:n, 0:256])
                    f = vf[k_ % 2]
                    k_ += 1
                    self.act(f[:n, :], bk[:n, 0:256], AF.Copy)
                    r0 = (tile0 + i) * P
                    self.dma(self.o_dv[kind][l, r0:r0 + n, h0 * P:(h0 + 2) * P], f[:n, :])
                    vb_ = vb16[k_ % 2]
                    self.cp(vb_[:n, :], f[:n, :])
                    self.dma(self.vbs[past + r0:past + r0 + n, h0 * P:(h0 + 2) * P], vb_[:n, :])
            elif b < 8:
                for cc in range(2):
                    bk = self.bank()
                    self.proj_fm(slot, cc * P, P, ntok, bk[:, 0:ntok])
                    self.act(self.zT[:, 2 * (b - 6) + cc, 0:ntok], bk[:, 0:ntok], AF.Silu)
            elif b < 12:
                for cc in range(2):
                    bk = self.bank()
                    self.proj_fm(slot, cc * P, P, ntok, bk[:, 0:ntok])
                    c_ = 2 * (b - 8) + cc
                    self.act(qz4[0:64, c_, 0, 0:ntok], bk[0:64, 0:ntok], AF.Copy)
                    self.act(qz4[64:128, c_, 1, 0:ntok], bk[64:128, 0:ntok], AF.Copy)
            else:
                for i in range(nt):
                    bk = self.bank()
                    self.proj_tm(slot, 80, i * P, n, bk[:n, 0:80])
                    f = kif[i % 2]
                    self.act(self.junk[:n, 0:64], bk[:n, 0:64], AF.Square, accum_out=gs[:n, 0:1])
                    self.rsqrt_(gs[:n, 4:5], gs[:n, 0:1], 1.0 / 64, EPS)
                    self.ts(f[:n, :], bk[:n, 0:64], gs[:n, 4:5], ALU.mult)
                    self.tt(f[:n, :], f[:n, :], self.lp[:n, L_IK:L_IK + 64], ALU.mult)
                    r0 = (tile0 + i) * P
                    self.dma(self.o_ik[kind][l, r0:r0 + n, :], f[:n, :])
                    self.cp(ki2[:n, 0:64], f[:n, :])
                    self.cp(ki2[:n, 64:128], f[:n, :])
                    tb = self.bank()
                    tbb = tb[:].bitcast(BF16)
                    self.tr(tbb[:, 0:n], ki2[:n, :], self.ident_b[:n, :n])
                    kpos = past + r0
                    self.cp(self.kiT2[:, kpos:kpos + n], tbb[:, 0:n], eng=self.fw.act)
                    self.act(absw[:n, i, :], bk[:n, 64:80], AF.Abs)
                    self.ts(sgn[:n, i, :], bk[:n, 64:80], 0.0, ALU.is_ge, 2.0, ALU.mult)
                    self.ts(sgn[:n, i, :], sgn[:n, i, :], -1.0, ALU.add)
        NIT = 16
        for i in range(nt):
            off = i * P
            j = tile0 + i
            S = past + (j + 1) * n if kind == 0 else past + n
            self.mark("indexer")
            self.tt(dsg[:n, :, 0:n], self.ident_b[:n, 0:n].unsqueeze(1).to_broadcast([n, 16, n]),
                    sgn[:n, i, :].unsqueeze(2).to_broadcast([n, 16, n]), ALU.mult)
            nch = (S + 511) // 512
            for c in range(nch):
                wd = min(512, S - 512 * c)
                ibid = self.bank_hold(1)
                ib = self.banks[ibid[0]]
                pend = []
                for h in range(16):
                    base = 64 * (h % 2)
                    ch = h // 2
                    sc = self.bank()
                    self.mm(sc[:n, 0:wd], qiZ[:, h, off:off + n], self.kiT2[:, 512 * c:512 * c + wd])
                    r = R[h % 6]
                    if h % 2 == 0:
                        self.act(r[:n, 0:wd], sc[:n, 0:wd], AF.Relu, scale=absw[:n, i, h:h + 1])
                    else:
                        self.ts(r[:n, 0:wd], sc[:n, 0:wd], absw[:n, i, h:h + 1], ALU.mult, 0.0, ALU.max)
                    pend.append((h, r))
                    if len(pend) > 2:
                        ph, pr = pend.pop(0)
                        self.mm(ib[:n, 0:wd], dsg[:n, ph, 0:n], pr[:n, 0:wd], start=(ph == 0), stop=False)
                while pend:
                    ph, pr = pend.pop(0)
                    self.mm(ib[:n, 0:wd], dsg[:n, ph, 0:n], pr[:n, 0:wd], start=(ph == 0), stop=(ph == 15))
                self.cp(index[:n, 512 * c:512 * c + wd], ib[:n, 0:wd], eng=(self.fw.act if c % 2 == 0 else self.fw.dve))
                self.bank_release(ibid)
            self.mark("bisect")
            mid, _u1, c1, _u2, v_, t_, mn_, mx_, thr0 = (bs[:n, k:k + 1] for k in range(9))
            nmid = bsa[:n, 0:1]
            sg = bsb[:n, 0:1]
            if S > TOPK:
                self.red(mx_, index[:n, 0:S], ALU.max)
                self.red(mn_, index[:n, 0:S], ALU.min)
                self.tt(t_, mx_, mn_, ALU.subtract)
                self.ts(t_, t_, 0.5, ALU.mult, 1.0, ALU.add)
                self.ts(bw[:n, 0:NIT + 1], self.p2[:n, 0:NIT + 1], t_, ALU.mult)
                self.ts(bw2[:n, 0:NIT + 1], bw[:n, 0:NIT + 1], 2.0, ALU.mult)
                self.tt(mid, mx_, mn_, ALU.add)
                self.ts(mid, mid, 0.5, ALU.mult)
                self.ts(nmid, mid, -1.0, ALU.mult)
            if kind == 0:
                self.mset(index[0:64, S - 64:S], NEG)
            if S > TOPK:
                Sd = (int(0.42 * S) // 8) * 8
                Sa = S - Sd
                for it in range(NIT):
                    self.ts(cjd[:n, 0:Sd], index[:n, 0:Sd], mid, ALU.is_gt, None, ALU.add, accum_out=c1)
                    self.act(cja[:n, 0:Sa], index[:n, Sd:S], AF.Sign, bias=nmid, accum_out=sg)
                    self.stt(v_, c1, 2.0, sg, ALU.mult, ALU.add)
                    self.stt(t_, v_, float(2 * TOPK - 1 - Sa), bw2[:n, it + 1:it + 2], ALU.is_ge, ALU.mult)
                    self.stt(mid, t_, bw[:n, it + 1:it + 2], mid, ALU.subtract, ALU.add)
                    self.stt(nmid, nmid, bw[:n, it + 1:it + 2], t_, ALU.add, ALU.subtract)
                thr = thr0
                self.tt(thr, mid, bw[:n, NIT:NIT + 1], ALU.subtract)
            else:
                thr = thr0
                self.mset(thr, -1.0e38)
            self.mark("attn")
            accs = self.bank_hold(4)
            nblk = (S + P - 1) // P
            npair = (nblk + 1) // 2

            def issue_load(kb2):
                slot = kb2 % 2
                k0 = kb2 * 256
                kw2 = min(256, S - k0)
                self.dma(Kx[slot][:, :, 0:kw2], self.kts[:, kb2, :, 0:kw2])
                for bq in range(2):
                    kb = 2 * kb2 + bq
                    if kb >= nblk:
                        break
                    kw = min(P, S - kb * P)
                    self.dma(Vx[slot][:kw, bq, :, 0:HD], self.vbs[kb * P:kb * P + kw, :].rearrange("s (h e) -> s h e", h=NH))

            def emit_pv(pv):
                kb, slot, bq, kw, PT_ = pv
                for h in range(NH):
                    self.mm(self.banks[accs[h]][:n, 0:HD + 1], PT_[:kw, h, 0:n], Vx[slot][:kw, bq, h, :],
                            start=(kb == 0), stop=(kb == nblk - 1))

            issue_load(0)
            pend = None
            for kb2 in range(npair):
                slot = kb2 % 2
                for bq in range(2):
                    kb = 2 * kb2 + bq
                    if kb >= nblk:
                        break
                    kw = min(P, S - kb * P)
                    sb_ = selb[kb % 3]
                    self.ts(sb_[:n, 0:kw], index[:n, kb * P:kb * P + kw], thr, ALU.is_le)
                    lg = self.bank()
                    lg3 = lg[:].rearrange("p (h f) -> p h f", h=NH)
                    for h in range(NH):
                        self.mm(lg3[:kw, h, 0:n], Kx[slot][:, h, bq * P:bq * P + kw], QcT[:, i, h, 0:n], start=(h == 0), stop=False)
                    self.mm(lg3[:kw, :, 0:n], sb_[:n, 0:kw], self.negI4[:n, :, 0:n], start=False, stop=True)
                    PT_ = PT[kb % 3]
                    self.act(PT_[:kw, :, 0:n], lg3[:kw, :, 0:n], AF.Exp, scale=SC)
                    if pend is not None:
                        emit_pv(pend)
                    if bq == 0 and kb2 + 1 < npair:
                        issue_load(kb2 + 1)
                    pend = (kb, slot, bq, kw, PT_)
            emit_pv(pend)
            for h in range(NH):
                a_ = self.banks[accs[h]]
                self.recip(gs[:n, h:h + 1], a_[:n, HD:HD + 1])
                self.ts(ob[:n, h, :], a_[:n, 0:HD], gs[:n, h:h + 1], ALU.mult)
            self.bank_release(accs)
            self.finish_heads(ob, n, off, 8)

    def out_proj(self, ctx):
        seq, nt, tile0 = ctx["seq"], ctx["nt"], ctx["tile0"]
        l, n = seq["l"], seq["n"]
        blocks = [(256 * b, 256) for b in range(8)]
        k_ = 0
        with ExitStack() as es:
            xfull = self.sb("xfull", [P, 4, D], F32, es)
            ycs = [self.sb("ycr", [P, 256], F32, es) for _ in range(6)]
            for i in range(nt):
                r0 = (tile0 + i) * P
                self.dma(xfull[:n, i, :], seq["xsrc"][r0:r0 + n, :], q=self.fw.act)
            for b, slot in self.wblocks(self.w_out[l], blocks):
                for i in range(nt):
                    r0 = (tile0 + i) * P
                    bk = self.bank()
                    self.proj_tm(slot, 256, i * P, n, bk[:n, 0:256], src=self.mixT)
                    yc = ycs[k_ % 6]
                    k_ += 1
                    self.tt(yc[:n, :], bk[:n, 0:256], xfull[:n, i, 256 * b:256 * b + 256], ALU.add)
                    self.dma(seq["ydst"][r0:r0 + n, 256 * b:256 * b + 256], yc[:n, :], q=self.fw.act)
            self.fw.barrier()


class nc_noncontig:
    def __init__(self, nc):
        self.cm = nc.allow_non_contiguous_dma(reason="small strided param/state transfers")

    def __enter__(self):
        return self.cm.__enter__()

    def __exit__(self, *a):
        return self.cm.__exit__(*a)


def make_consts():
    c = np.zeros((P, NCST), np.float32)
    p = np.arange(P)[:, None]
    f = np.arange(P)[None, :]
    c[:, C_ID:C_ID + P] = (p == f)
    c[:, C_ONE:C_ONE + P] = 1.0
    c[:, C_U:C_U + P] = (p <= f)
    c[:, C_MNEG:C_MNEG + P] = -1.0 * (p > f)
    c[:, C_MUP:C_MUP + P] = SC * (f >= p)
    for h in range(NH):
        gam = 1.0 - 2.0 ** (-5.0 - h)
        c[:, C_RDM + h * P:C_RDM + (h + 1) * P] = (gam ** (-(p + 1.0))) * SC * (f >= p)
        c[:, C_CD + h] = gam ** (np.arange(P) + 1.0)
        c[:, C_SD128 + h] = gam ** (127.0 - np.arange(P)) * SC
        c[:32, C_SD32 + h] = gam ** (31.0 - np.arange(32)) * SC
    return c


def make_rope(T_, TS, PAST):
    half = 64
    inv = (10000.0 ** (-np.arange(half, dtype=np.float32) / half)).astype(np.float32)
    pos = np.concatenate([np.arange(T_), PAST + np.arange(TS)]).astype(np.float32)
    ang = (pos[:, None] * inv[None, :]).astype(np.float32)
    return np.concatenate([np.cos(ang), np.sin(ang)], axis=1).astype(np.float32)


def make_lp(inp, DEPTH):
    lp = np.zeros((DEPTH, P, NLP), np.float32)
    for l in range(DEPTH):
        lp[l, :, L_G:L_G + 16] = inp["norm_g"][l].reshape(16, P).T
        lp[l, :, L_MG:L_MG + 16] = inp["mem_norm_g"][l].reshape(16, P).T
        cw = inp["gdn_conv_w"][l]
        lp[l, :, L_CW:L_CW + 48] = cw.reshape(4, 12, P).transpose(2, 1, 0).reshape(P, 48)
        lp[l, :, L_AL:L_AL + 4] = inp["gdn_a_log"][l][None, :]
        lp[l, :, L_DT:L_DT + 4] = inp["gdn_dt_bias"][l][None, :]
        lp[l, :, L_GDN] = inp["gdn_norm_g"][l]
        lp[l, :, L_RG] = inp["ret_norm_g"][l]
        lp[l, :, L_RB] = inp["ret_norm_b"][l]
        lp[l, :, L_DQ] = inp["dsa_q_norm_g"][l]
        lp[l, :, L_MQ] = inp["mem_q_norm_g"][l]
        lp[l, :, L_DK:L_DK + P] = inp["dsa_k_norm_g"][l][None, :]
        lp[l, :, L_MK:L_MK + P] = inp["mem_k_norm_g"][l][None, :]
        lp[l, :, L_IK:L_IK + 64] = inp["idx_k_norm_g"][l][None, :]
    return lp


def run(inp, T_, TS, PAST, DEPTH, dbg_cols=0, ncores=8):
    inp = {k: np.asarray(v) for k, v in inp.items()}
    bld = Builder(T_, TS, PAST, DEPTH, dbg_cols)
    nc = bld.build()
    cst = make_consts()
    rope = make_rope(T_, TS, PAST)
    lp = make_lp(inp, DEPTH)
    ca = np.ascontiguousarray
    w_in, w_mkv, w_out = ca(inp["w_in"]), ca(inp["w_mem_kv"]), ca(inp["w_out"])
    NB = inp["x_prompt"].shape[0]
    NSB = inp["x_sample"].shape[0]
    in_maps = []
    for c in range(ncores):
        pb = (c // 2) % NB
        sbi = c % NSB
        in_maps.append(dict(
            x_p=ca(inp["x_prompt"][pb]), x_s=ca(inp["x_sample"][sbi]),
            conv_s=ca(inp["cache_gdn_conv"][:, sbi]), sg_s=ca(inp["state_gdn"][:, sbi]), sr_s=ca(inp["state_ret"][:, sbi]),
            ck_s=ca(inp["cache_dsa_k"][:, sbi].reshape(DEPTH, PAST, 512)), cv_s=ca(inp["cache_dsa_v"][:, sbi].reshape(DEPTH, PAST, 512)),
            cik_s=ca(inp["cache_idx_k"][:, sbi]), cmk_s=ca(inp["cache_mem_k"][:, sbi].reshape(DEPTH, NMEM, 512)),
            cmv_s=ca(inp["cache_mem_v"][:, sbi].reshape(DEPTH, NMEM, 512)), mem_p=ca(inp["mem_prompt"][pb]),
            w_in=w_in, w_mkv=w_mkv, w_out=w_out, lp=lp, cst=cst, rope=rope))
    res = run_bass_kernel_spmd(nc, in_maps, core_ids=list(range(ncores)))
    R = res.results

    def stack_p(name, shp):
        return np.stack([np.asarray(R[2 * b][name]).reshape(shp) for b in range(NB)], axis=1)

    def stack_s(name, shp):
        return np.stack([np.asarray(R[c][name]).reshape(shp) for c in range(NSB)], axis=1)

    y_p = np.stack([np.asarray(R[2 * b]["y_p"]) for b in range(NB)], axis=0)
    y_s = np.stack([np.asarray(R[c]["y_s"]) for c in range(NSB)], axis=0)
    outs = (
        y_p, y_s,
        stack_p("o_conv_p", (DEPTH, 3, 1536)), stack_p("o_gdn_p", (DEPTH, NH, HD, HD)), stack_p("o_ret_p", (DEPTH, NH, HD, HD)),
        stack_p("o_dk_p", (DEPTH, T_, NH, HD)), stack_p("o_dv_p", (DEPTH, T_, NH, HD)), stack_p("o_ik_p", (DEPTH, T_, 64)),
        stack_p("o_mk_p", (DEPTH, NMEM, NH, HD)), stack_p("o_mv_p", (DEPTH, NMEM, NH, HD)),
        stack_s("o_conv_s", (DEPTH, 3, 1536)), stack_s("o_gdn_s", (DEPTH, NH, HD, HD)), stack_s("o_ret_s", (DEPTH, NH, HD, HD)),
        stack_s("o_dk_s", (DEPTH, TS, NH, HD)), stack_s("o_dv_s", (DEPTH, TS, NH, HD)), stack_s("o_ik_s", (DEPTH, TS, 64)),
    )
    outs = tuple(np.ascontiguousarray(o, dtype=np.float32) for o in outs)
    if dbg_cols:
        return outs, [np.asarray(R[c]["dbg"]) for c in range(ncores)]
    return outs


def kernel(**inputs):
    return run(inputs, 8192, 32, 4096, 2)
```

```python
import math
from contextlib import ExitStack

import numpy as np
import concourse.bass as bass
import concourse.mybir as mybir
from concourse.bass_utils import run_bass_kernel_spmd

F32 = mybir.dt.float32
BF16 = mybir.dt.bfloat16
AF = mybir.ActivationFunctionType
ALU = mybir.AluOpType
AX = mybir.AxisListType

P = 128
D = 2048
KCN = 16
HD = 128
NH = 4
QA, KA, VA, ZA, BA, AA = 0, 512, 1024, 1536, 2048, 2052
QB, KB, VB, ZB = 2056, 2568, 3080, 3592
QC, KC, VC, ZC = 4104, 4616, 5128, 5640
QI, KI, WI, QD, ZD = 6152, 7176, 7240, 7256, 7768
INC = 8280
EPS = 1e-6
NEG = -3.0e38
NMEM = 256
TOPK = 256
SC = HD ** -0.5

C_ID, C_ONE, C_U, C_MNEG, C_MUP, C_RDM, C_CD, C_SD128, C_SD32, NCST = 0, 128, 256, 384, 512, 640, 1152, 1156, 1160, 1164
L_G, L_MG, L_CW, L_AL, L_DT, L_GDN, L_RG, L_RB, L_DQ, L_MQ, L_DK, L_MK, L_IK, NLP = 0, 16, 32, 80, 84, 88, 89, 90, 91, 92, 93, 221, 349, 413


class Trk:
    __slots__ = ("w", "r", "dram", "ws")

    def __init__(self, dram=False):
        self.w = None
        self.r = {}
        self.dram = dram
        self.ws = {}


class Eng:
    def __init__(self, nc, name, e, is_pe=False):
        self.name = name
        self.e = e
        self.sem = nc.alloc_semaphore("s_" + name)
        self.key = "E_" + name
        self.cnt = 0
        self.wm = {}
        self.is_pe = is_pe


class FW:
    def __init__(self, n_dma_sems=32, same_engine_sync=True):
        self.nc = bass.Bass("TRN2", target_bir_lowering=False)
        nc = self.nc
        self.pe = Eng(nc, "pe", nc.tensor, True)
        self.act = Eng(nc, "act", nc.scalar)
        self.dve = Eng(nc, "dve", nc.vector)
        self.pool = Eng(nc, "pool", nc.gpsimd)
        self.sp = Eng(nc, "sp", nc.sync)
        self.engs = [self.pe, self.act, self.dve, self.pool, self.sp]
        self.dsems = [nc.alloc_semaphore("d%d" % i) for i in range(n_dma_sems)]
        self.dcnt = [0] * n_dma_sems
        self.dpers = [True] * n_dma_sems
        self.dnext = 0
        self.dnext_sw = 0
        self.n_sw = 8
        self.same_engine_sync = same_engine_sync
        self.n_ins = 0
        self.n_wait = 0

    def _waits(self, E, reads, writes):
        need = {}

        def add(k, s, v):
            o = need.get(k)
            if o is None or o[1] < v:
                need[k] = (s, v)
        for t in reads:
            if t.dram:
                for k, (s, v) in t.ws.items():
                    add(k, s, v)
            elif t.w is not None:
                add(*t.w)
        for t in writes:
            if (not t.dram) and t.w is not None:
                add(*t.w)
            for k, (s, v) in t.r.items():
                add(k, s, v)
        for k, (s, v) in need.items():
            if k == E.key and (E.is_pe or not self.same_engine_sync):
                continue
            if E.wm.get(k, 0) >= v:
                continue
            E.e.wait_ge(s, v)
            E.wm[k] = v
            self.n_wait += 1

    @staticmethod
    def _commit(tok, reads, writes):
        k, s, v = tok
        for t in reads:
            o = t.r.get(k)
            if o is None or o[1] < v:
                t.r[k] = (s, v)
        for t in writes:
            if t.dram:
                o = t.ws.get(k)
                if o is None or o[1] < v:
                    t.ws[k] = (s, v)
            else:
                t.w = tok
                t.r = {}

    def op(self, E, fn, reads, writes):
        self._waits(E, reads, writes)
        ins = fn()
        E.cnt += 1
        ins.then_inc(E.sem, 1)
        self.n_ins += 1
        self._commit((E.key, E.sem, E.cnt), reads, writes)

    def dma(self, E, out, in_, reads, writes, persistent=False, **kw):
        self._waits(E, reads, writes)
        if E is self.pool:
            i = self.dnext_sw
            self.dnext_sw = (self.dnext_sw + 1) % self.n_sw
        else:
            i = self.n_sw + self.dnext
            self.dnext = (self.dnext + 1) % (len(self.dsems) - self.n_sw)
        s = self.dsems[i]
        k = "D%d" % i
        prev = 16 * self.dcnt[i]
        if prev > 0 and E.wm.get(k, 0) < prev:
            E.e.wait_ge(s, prev)
            E.wm[k] = prev
            self.n_wait += 1
        E.e.dma_start(out=out, in_=in_, **kw).then_inc(s, 16)
        self.dcnt[i] += 1
        self.dpers[i] = persistent
        self.n_ins += 1
        self._commit((k, s, 16 * self.dcnt[i]), reads, writes)

    def barrier(self):
        for E in self.engs:
            for i, s in enumerate(self.dsems):
                v = 16 * self.dcnt[i]
                k = "D%d" % i
                if v > 0 and (not self.dpers[i]) and E.wm.get(k, 0) < v:
                    E.e.wait_ge(s, v)
                    E.wm[k] = v
                    self.n_wait += 1
            for X in self.engs:
                if X is E or X.cnt == 0:
                    continue
                if E.wm.get(X.key, 0) < X.cnt:
                    E.e.wait_ge(X.sem, X.cnt)
                    E.wm[X.key] = X.cnt
                    self.n_wait += 1

    def finish(self):
        E = self.sp
        for i, s in enumerate(self.dsems):
            v = 16 * self.dcnt[i]
            if v > 0:
                E.e.wait_ge(s, v)
        for X in self.engs:
            if X is not E and X.cnt > 0:
                E.e.wait_ge(X.sem, X.cnt)


class Builder:
    def __init__(self, T_, TS, PAST, DEPTH, dbg_cols=0):
        self.T_, self.TS, self.PAST, self.DEPTH = T_, TS, PAST, DEPTH
        self.fw = FW()
        self.nc = self.fw.nc
        self.reg = {}
        self.uid = 0
        self.dbg_cols = dbg_cols
        self.dbg_off = 0
        self.marks = []

    def mark(self, label):
        self.marks.append((label, self.fw.pe.cnt))

    def _name(self, n):
        self.uid += 1
        return "%s_%d" % (n, self.uid)

    def sb(self, n, shape, dt=F32, es=None):
        name = self._name(n)
        if es is None:
            t = self.nc.alloc_sbuf_tensor(name, list(shape), dt)
        else:
            t = es.enter_context(self.nc.sbuf_tensor(name, list(shape), dt))
        self.reg[name] = Trk()
        return t

    def din(self, n, shape, dt=F32):
        t = self.nc.dram_tensor(n, list(shape), dt, kind="ExternalInput").ap()
        self.reg[n] = Trk(dram=True)
        return t

    def dout(self, n, shape, dt=F32):
        t = self.nc.dram_tensor(n, list(shape), dt, kind="ExternalOutput").ap()
        self.reg[n] = Trk(dram=True)
        return t

    def dscr(self, n, shape, dt=F32):
        t = self.nc.dram_tensor(n, list(shape), dt, kind="Internal").ap()
        self.reg[n] = Trk(dram=True)
        return t

    def _t(self, *aps):
        out = []
        for a in aps:
            if hasattr(a, "tensor"):
                out.append(self.reg[a.tensor.name])
        return out

    def init_banks(self):
        self.banks = []
        for i in range(8):
            name = "bank%d" % i
            t = self.nc.alloc_psum_tensor(name, [P, 512], F32)
            self.reg[name] = Trk()
            self.banks.append(t)
        self.bank_free = list(range(8))
        self.bank_rr = 0

    def bank(self):
        i = self.bank_free[self.bank_rr % len(self.bank_free)]
        self.bank_rr += 1
        return self.banks[i]

    def bank_hold(self, k):
        got = []
        for _ in range(k):
            i = self.bank_free[self.bank_rr % len(self.bank_free)]
            self.bank_free.remove(i)
            got.append(i)
        return got

    def bank_release(self, ids):
        for i in ids:
            self.bank_free.append(i)
        self.bank_free.sort()

    def mm(self, out, lhsT, rhs, start=True, stop=True):
        nc = self.nc
        self.fw.op(self.fw.pe, lambda: nc.tensor.matmul(out, lhsT=lhsT, rhs=rhs, start=start, stop=stop),
                   self._t(lhsT, rhs), self._t(out))

    def tr(self, out, in_, ident):
        nc = self.nc
        self.fw.op(self.fw.pe, lambda: nc.tensor.transpose(out, in_, ident), self._t(in_, ident), self._t(out))

    def act(self, out, in_, func, bias=None, scale=None, accum_out=None):
        nc = self.nc
        kw = {}
        if bias is not None:
            kw["bias"] = bias
        if scale is not None:
            kw["scale"] = scale
        if accum_out is not None:
            kw["accum_out"] = accum_out
        self.fw.op(self.fw.act, lambda: nc.scalar.activation(out=out, in_=in_, func=func, **kw),
                   self._t(in_, bias, scale), self._t(out, accum_out))

    def ts(self, out, in0, s1, op0, s2=None, op1=None, accum_out=None, eng=None):
        nc = self.nc
        E = eng or self.fw.dve
        kw = {}
        if op1 is not None:
            kw["op1"] = op1
        if accum_out is not None:
            kw["accum_out"] = accum_out
        self.fw.op(E, lambda: E.e.tensor_scalar(out=out, in0=in0, scalar1=s1, scalar2=s2, op0=op0, **kw),
                   self._t(in0, s1, s2), self._t(out, accum_out))

    def tt(self, out, in0, in1, op, eng=None):
        E = eng or self.fw.dve
        self.fw.op(E, lambda: E.e.tensor_tensor(out=out, in0=in0, in1=in1, op=op), self._t(in0, in1), self._t(out))

    def stt(self, out, in0, scalar, in1, op0, op1):
        nc = self.nc
        self.fw.op(self.fw.dve, lambda: nc.vector.scalar_tensor_tensor(out=out, in0=in0, scalar=scalar, in1=in1, op0=op0, op1=op1),
                   self._t(in0, scalar, in1), self._t(out))

    def red(self, out, in_, op, axis=AX.X):
        nc = self.nc
        self.fw.op(self.fw.dve, lambda: nc.vector.tensor_reduce(out=out, in_=in_, axis=axis, op=op), self._t(in_), self._t(out))

    def recip(self, out, in_):
        nc = self.nc
        self.fw.op(self.fw.dve, lambda: nc.vector.reciprocal(out=out, in_=in_), self._t(in_), self._t(out))

    def cp(self, out, in_, eng=None):
        E = eng or self.fw.dve
        if E is self.fw.act:
            return self.act(out, in_, AF.Copy)
        self.fw.op(E, lambda: E.e.tensor_copy(out=out, in_=in_), self._t(in_), self._t(out))

    def mset(self, out, val, eng=None):
        E = eng or self.fw.dve
        self.fw.op(E, lambda: E.e.memset(out, val), [], self._t(out))

    def dma(self, out, in_, q=None, **kw):
        E = q or self.fw.sp
        self.fw.dma(E, out, in_, self._t(in_), self._t(out), **kw)

    def rsqrt_(self, out, in_, mul, eps):
        self.ts(out, in_, mul, ALU.mult, eps, ALU.add)
        self.act(out, out, AF.Sqrt)
        self.recip(out, out)

    def dbg(self, ap, rows, cols):
        if not self.dbg_cols:
            return
        self.dma(self.dbgt[0:rows, self.dbg_off:self.dbg_off + cols], ap)
        self.dbg_off += cols

    def build(self):
        T_, TS, PAST, DEPTH = self.T_, self.TS, self.PAST, self.DEPTH
        nc = self.nc
        SK = PAST + TS
        self.SMAX = max(T_, SK)
        self.x_p = self.din("x_p", [T_, D])
        self.x_s = self.din("x_s", [TS, D])
        self.conv_s = self.din("conv_s", [DEPTH, 3, 1536])
        self.sg_s = self.din("sg_s", [DEPTH, NH, HD, HD])
        self.sr_s = self.din("sr_s", [DEPTH, NH, HD, HD])
        self.ck_s = self.din("ck_s", [DEPTH, PAST, 512])
        self.cv_s = self.din("cv_s", [DEPTH, PAST, 512])
        self.cik_s = self.din("cik_s", [DEPTH, PAST, 64])
        self.cmk_s = self.din("cmk_s", [DEPTH, NMEM, 512])
        self.cmv_s = self.din("cmv_s", [DEPTH, NMEM, 512])
        self.mem_p = self.din("mem_p", [NMEM, D])
        self.w_in = self.din("w_in", [DEPTH, D, INC])
        self.w_mkv = self.din("w_mkv", [DEPTH, D, 1024])
        self.w_out = self.din("w_out", [DEPTH, D, D])
        self.lp_d = self.din("lp", [DEPTH, P, NLP])
        self.cst_d = self.din("cst", [P, NCST])
        self.rope_d = self.din("rope", [T_ + TS, 128])

        self.y_p = self.dout("y_p", [T_, D])
        self.y_s = self.dout("y_s", [TS, D])
        self.o_conv = [self.dout("o_conv_p", [DEPTH, 3, 1536]), self.dout("o_conv_s", [DEPTH, 3, 1536])]
        self.o_gdn = [self.dout("o_gdn_p", [DEPTH, NH, HD, HD]), self.dout("o_gdn_s", [DEPTH, NH, HD, HD])]
        self.o_ret = [self.dout("o_ret_p", [DEPTH, NH, HD, HD]), self.dout("o_ret_s", [DEPTH, NH, HD, HD])]
        self.o_dk = [self.dout("o_dk_p", [DEPTH, T_, 512]), self.dout("o_dk_s", [DEPTH, TS, 512])]
        self.o_dv = [self.dout("o_dv_p", [DEPTH, T_, 512]), self.dout("o_dv_s", [DEPTH, TS, 512])]
        self.o_ik = [self.dout("o_ik_p", [DEPTH, T_, 64]), self.dout("o_ik_s", [DEPTH, TS, 64])]
        self.o_mk = self.dout("o_mk_p", [DEPTH, NMEM, 512])
        self.o_mv = self.dout("o_mv_p", [DEPTH, NMEM, 512])
        if self.dbg_cols:
            self.dbgt = self.dout("dbg", [P, self.dbg_cols])
        self.y1_p = self.dscr("y1_p", [T_, D])
        self.y1_s = self.dscr("y1_s", [TS, D])
        self.kts = self.dscr("kts", [HD, (self.SMAX + 255) // 256, NH, 256], BF16)
        self.vbs = self.dscr("vbs", [self.SMAX, 512], BF16)
        c0s = ([256 * i for i in range(8)] + [BA] + [QB + 256 * i for i in range(8)] + [QC + 256 * i for i in range(8)]
               + [QI + 256 * i for i in range(4)] + [KI] + [QD + 256 * i for i in range(4)])
        self.wblk = {}
        self.wq = {}
        for l in range(DEPTH):
            for nm, src_, cl in (("w_in", self.w_in, c0s), ("w_out", self.w_out, [256 * i for i in range(8)]),
                                 ("w_mkv", self.w_mkv, [256 * i for i in range(4)])):
                t = self.dscr("q_%s_%d" % (nm, l), [len(cl), P, KCN, 256], BF16)
                self.wq[(nm, l)] = (t, src_[l], {c: i for i, c in enumerate(cl)})
        self.conv_pending = []
        self.conv_trk = Trk()

        self.init_banks()
        sb = self.sb
        self.cst = sb("cst", [P, NCST])
        self.ident_b = sb("identb", [P, P], BF16)
        self.negI = sb("negI", [P, P], BF16)
        self.negI4 = sb("negI4", [P, NH, P], BF16)
        self.ones_b = sb("onesb", [P, P], BF16)
        self.lp = sb("lpt", [P, NLP])
        self.nA = sb("nA", [P, NH])
        self.xnT = sb("xnT", [P, KCN, 512], BF16)
        self.mixT = sb("mixT", [P, KCN, 512], BF16)
        self.wr = [sb("wr%d" % i, [P, KCN, 256], BF16) for i in range(3)]
        self.wr_i = 0
        self.pref = None
        self.zT = sb("zT", [P, NH, 512], BF16)
        self.S_g = sb("S_g", [P, NH, HD])
        self.Sb_g = sb("Sb_g", [P, NH, HD], BF16)
        self.S_r = sb("S_r", [P, NH, HD])
        self.Sb_r = sb("Sb_r", [P, NH, HD], BF16)
        self.hist = sb("hist", [P, 12, 3])
        self.kiT2 = sb("kiT2", [P, self.SMAX], BF16)
        self.mkT = sb("mkT", [P, NH, NMEM], BF16)
        self.mvx = sb("mvx", [P, 2, NH, HD + 1], BF16)
        self.sm = sb("sm", [P, 64])
        self.junk = sb("junk", [P, P], BF16)
        self.junk4 = sb("junk4", [P, 4, P], BF16)
        self.qb24 = sb("qb24", [P, 4, 2, P], BF16)
        self.cc = sb("cc", [P, 2])
        self.p2 = sb("p2", [P, 20])
        self.xc = [sb("xc%d" % i, [P, 256]) for i in range(4)]
        self.yc = [sb("yc%d" % i, [P, 256]) for i in range(3)]

        self.ident_f = self.cst[:, C_ID:C_ID + P]
        self.ones_f = self.cst[:, C_ONE:C_ONE + P]
        self.U_f = self.cst[:, C_U:C_U + P]
        self.Mneg = self.cst[:, C_MNEG:C_MNEG + P]
        self.Mup = self.cst[:, C_MUP:C_MUP + P]
        self.retDm = self.cst[:, C_RDM:C_RDM + 512].rearrange("p (h f) -> p h f", h=NH)

        self.dma(self.cst[:], self.cst_d[:, :])
        self.cp(self.ident_b[:], self.ident_f)
        self.cp(self.ident_b[:], self.ident_f)
        self.cp(self.ones_b[:], self.ones_f)
        self.ts(self.negI[:], self.ident_f, -30000.0, ALU.mult)
        for h in range(NH):
            self.ts(self.negI4[:, h, :], self.ident_f, -30000.0, ALU.mult)
        self.mset(self.mvx[:], 1.0)
        self.mset(self.cc[:, 0:1], EPS)
        for k in range(20):
            self.mset(self.p2[:, k:k + 1], 2.0 ** (-k))
        self.mset(self.cc[:, 1:2], 1.0)
        self.epsc = self.cc[:, 0:1]
        self.onec = self.cc[:, 1:2]
        self.mset(self.mixT[:], 0.0)

        self.convert_weights(0)
        for l in range(DEPTH):
            self.cur_l = l
            if l + 1 < DEPTH:
                self.convert_weights(l + 1, limit=0)
            self.layer_params(l)
            seq = dict(kind=0, n=P, ntiles=T_ // P, st_tiles=4, l=l,
                       xsrc=self.x_p if l == 0 else self.y1_p,
                       ydst=self.y_p if l == DEPTH - 1 else self.y1_p, pos0=0, rope0=0, past=0)
            self.run_sequence(seq)
            seq = dict(kind=1, n=TS, ntiles=1, st_tiles=1, l=l,
                       xsrc=self.x_s if l == 0 else self.y1_s,
                       ydst=self.y_s if l == DEPTH - 1 else self.y1_s, pos0=PAST, rope0=T_, past=PAST)
            self.run_sequence(seq)
        self.fw.finish()
        return nc

    def layer_params(self, l):
        self.dma(self.lp[:], self.lp_d[l, :, :])
        self.act(self.nA[:], self.lp[:, L_AL:L_AL + NH], AF.Exp)
        self.ts(self.nA[:], self.nA[:], -1.0, ALU.mult)

    def convert_weights(self, l, limit=None):
        for nm in ("w_mkv", "w_in", "w_out"):
            t, w, idx = self.wq[(nm, l)]
            width = {"w_in": INC, "w_out": D, "w_mkv": 1024}[nm]
            cs = sorted(idx.keys())
            for j, c0 in enumerate(cs):
                c1 = cs[j + 1] if j + 1 < len(cs) else width
                self.conv_pending.append((t[idx[c0], :, :, 0:c1 - c0], w[:, c0:c1].rearrange("(kc p) n -> p kc n", p=P)))
        self.flush_conv(limit)

    def flush_conv(self, limit=None):
        k = 0
        while self.conv_pending and (limit is None or k < limit):
            o, i = self.conv_pending.pop(0)
            self.fw.dma(self.fw.pool, o, i, self._t(i), self._t(o) + [self.conv_trk], persistent=True)
            k += 1

    def load_w(self, wsrc, c0, ncols):
        key = (wsrc.tensor.name, wsrc.offset, c0, ncols)
        if self.pref is not None and self.pref[0] == key:
            slot = self.pref[1]
            self.pref = None
            return slot
        slot = self.wr[self.wr_i % 3]
        self.wr_i += 1
        nm = {"w_in": "w_in", "w_out": "w_out", "w_mkv": "w_mkv"}[wsrc.tensor.name]
        t, w, idx = self.wq[(nm, self.cur_l)]
        self.dma(slot[:, :, 0:ncols], t[idx[c0], :, :, 0:ncols], persistent=True)
        return slot

    def prefetch_w(self, wsrc, c0, ncols):
        assert self.pref is None
        key = (wsrc.tensor.name, wsrc.offset, c0, ncols)
        slot = self.load_w(wsrc, c0, ncols)
        self.pref = (key, slot)

    def wblocks(self, wsrc, blocks):
        nxt = self.load_w(wsrc, blocks[0][0], blocks[0][1])
        for b in range(len(blocks)):
            cur = nxt
            if b + 1 < len(blocks):
                nxt = self.load_w(wsrc, blocks[b + 1][0], blocks[b + 1][1])
            yield b, cur

    def proj_fm(self, slot, cc0, m, ntok, out):
        for kc in range(KCN):
            self.mm(out, slot[:, kc, cc0:cc0 + m], self.xnT[:, kc, 0:ntok], start=(kc == 0), stop=(kc == KCN - 1))

    def proj_tm(self, slot, ncols, off, n, out, src=None):
        src = src if src is not None else self.xnT
        for kc in range(KCN):
            self.mm(out, src[:, kc, off:off + n], slot[:, kc, 0:ncols], start=(kc == 0), stop=(kc == KCN - 1))

    def norm_rows_to_T(self, src_rows, n, gcol, dstT, off, bi=0):
        sm = self.smx[bi]
        xt, xnb = self.xt[bi], self.xnb[bi]
        self.dma(xt[:n, :], src_rows)
        self.act(xnb[:n, :], xt[:n, :], AF.Square, accum_out=sm[:n, 0:1])
        self.rsqrt_(sm[:n, 0:1], sm[:n, 0:1], 1.0 / D, EPS)
        self.ts(xnb[:n, :], xt[:n, :], sm[:n, 0:1], ALU.mult)
        for half in range(2):
            bk = self.bank()
            bkb = bk[:].bitcast(BF16).rearrange("p (a b) -> p a b", a=8)
            for j in range(8):
                kc = half * 8 + j
                self.tr(bkb[:, j, 0:n], xnb[:n, kc * P:(kc + 1) * P], self.ident_b[:n, :n])
            g = self.lp[:, gcol + half * 8:gcol + half * 8 + 8].unsqueeze(2).to_broadcast([P, 8, n])
            self.tt(dstT[:, half * 8:half * 8 + 8, off:off + n], bkb[:, :, 0:n], g, ALU.mult)

    def run_sequence(self, seq):
        l, n, kind = seq["l"], seq["n"], seq["kind"]
        T_, PAST = self.T_, self.PAST
        fw = self.fw
        if kind == 0:
            self.mset(self.S_g[:], 0.0)
            self.mset(self.S_r[:], 0.0)
            self.mset(self.hist[:], 0.0)
            self.memory_kv(l)
        else:
            self.dma(self.S_g[:], self.sg_s[l].rearrange("h d e -> d h e"))
            self.dma(self.S_r[:], self.sr_s[l].rearrange("h d e -> d h e"))
            with nc_noncontig(self.nc):
                for c in range(12):
                    self.dma(self.hist[:, c, :], self.conv_s[l, :, c * P:(c + 1) * P].rearrange("j p -> p j"))
            self.sample_caches(l)
        self.cp(self.Sb_g[:], self.S_g[:], eng=fw.act)
        self.cp(self.Sb_r[:], self.S_r[:], eng=fw.act)

        nst = (seq["ntiles"] + seq["st_tiles"] - 1) // seq["st_tiles"]
        for st in range(nst):
            nt = min(seq["st_tiles"], seq["ntiles"] - st * seq["st_tiles"])
            ntok = nt * n if n == P else n
            tile0 = st * seq["st_tiles"]
            self.mark("phaseX")
            fw.barrier()
            with ExitStack() as esx:
                self.xt = [self.sb("xt", [P, D], F32, esx) for _ in range(2)]
                self.xnb = [self.sb("xnb", [P, D], BF16, esx) for _ in range(2)]
                self.smx = [self.sb("smx", [P, 2], F32, esx) for _ in range(2)]
                for i in range(nt):
                    r0 = (tile0 + i) * P
                    self.norm_rows_to_T(seq["xsrc"][r0:r0 + n, :], n, L_G, self.xnT, i * P, bi=i % 2)
                fw.barrier()
            ctx = dict(seq=seq, st=st, nt=nt, ntok=ntok, tile0=tile0)
            firsts = [(self.w_in[l], QA, 256), (self.w_in[l], QB, 256), (self.w_in[l], QC, 256), (self.w_in[l], QD, 256), (self.w_out[l], 0, 256)]
            for gi, grp in enumerate((self.group_a, self.group_b, self.group_c, self.group_d)):
                if self.pref is None:
                    self.prefetch_w(*firsts[gi])
                fw.barrier()
                with ExitStack() as es:
                    grp(ctx, es)
                    self.prefetch_w(*firsts[gi + 1])
                    fw.barrier()
            self.mark("outproj")
            self.out_proj(ctx)
            self.flush_conv(4)
            self.mark("end_st")
            if st + 1 < nst:
                self.prefetch_w(*firsts[0])
        if kind == 1:
            self.flush_conv(None)
        self.dma(self.o_gdn[kind][l].rearrange("h d e -> d h e"), self.S_g[:])
        self.dma(self.o_ret[kind][l].rearrange("h d e -> d h e"), self.S_r[:])
        with nc_noncontig(self.nc):
            for c in range(12):
                self.dma(self.o_conv[kind][l, :, c * P:(c + 1) * P].rearrange("j p -> p j"), self.hist[:, c, :])

    def finish_heads(self, ob, n, off, base, gain=None, bias=None):
        bk = self.bank()
        bkb = bk[:].bitcast(BF16).rearrange("p (a b) -> p a b", a=8)
        for h in range(NH):
            self.tr(bkb[:, h, 0:n], ob[:n, h, :], self.ident_b[:n, :n])
        dst = self.mixT[:, base:base + NH, off:off + n]
        z = self.zT[:, :, off:off + n]
        if gain is None:
            self.tt(dst, bkb[:, 0:NH, 0:n], z, ALU.mult)
        elif bias is None:
            self.stt(dst, bkb[:, 0:NH, 0:n], gain, z, ALU.mult, ALU.mult)
        else:
            self.act(dst, bkb[:, 0:NH, 0:n], AF.Identity, bias=bias, scale=gain)
            self.tt(dst, dst, z, ALU.mult)

    def z_blocks(self, gen_pairs, ntok):
        for (b, slot), cbase in gen_pairs:
            for cc in range(2):
                bk = self.bank()
                self.proj_fm(slot, cc * P, P, ntok, bk[:, 0:ntok])
                self.act(self.zT[:, cbase + cc, 0:ntok], bk[:, 0:ntok], AF.Silu)

    def group_a(self, ctx, es):
        self.mark("projA")
        seq, nt, ntok = ctx["seq"], ctx["nt"], ctx["ntok"]
        l, n = seq["l"], seq["n"]
        sb = self.sb
        w = self.w_in[l]
        qkvT = sb("qkvT", [P, 12, 512], BF16, es)
        xp = [sb("xp", [P, 515], F32, es) for _ in range(2)]
        acc = [sb("acc", [P, 512], F32, es) for _ in range(2)]
        ba = sb("ba", [P, 4, 8], F32, es)
        blocks = [(QA + 256 * i, 256) for i in range(6)] + [(ZA, 256), (ZA + 256, 256), (BA, 8)]
        cw = self.lp
        for b, slot in self.wblocks(w, blocks):
            if b < 6:
                for cc in range(2):
                    c = 2 * b + cc
                    bk = self.bank()
                    self.proj_fm(slot, cc * P, P, ntok, bk[:, 0:ntok])
                    x_ = xp[c % 2]
                    a_ = acc[c % 2]
                    self.act(x_[:, 3:3 + ntok], bk[:, 0:ntok], AF.Copy)
                    self.cp(x_[:, 0:3], self.hist[:, c, :])
                    self.ts(a_[:, 0:ntok], x_[:, 0:ntok], cw[:, L_CW + 4 * c:L_CW + 4 * c + 1], ALU.mult)
                    for j in range(1, 4):
                        self.stt(a_[:, 0:ntok], x_[:, j:j + ntok], cw[:, L_CW + 4 * c + j:L_CW + 4 * c + j + 1], a_[:, 0:ntok], ALU.mult, ALU.add)
                    self.cp(self.hist[:, c, :], x_[:, ntok:ntok + 3])
                    self.act(qkvT[:, c, 0:ntok], a_[:, 0:ntok], AF.Silu)
            elif b < 8:
                for cc in range(2):
                    bk = self.bank()
                    self.proj_fm(slot, cc * P, P, ntok, bk[:, 0:ntok])
                    self.act(self.zT[:, 2 * (b - 6) + cc, 0:ntok], bk[:, 0:ntok], AF.Silu)
            else:
                for i in range(nt):
                    bk = self.bank()
                    self.proj_tm(slot, 8, i * P, n, bk[:n, 0:8])
                    self.act(ba[:n, i, :], bk[:n, 0:8], AF.Copy)
        self.mark("gdn")
        NSLOT = 4
        gb = self.gdn_bufs(es, NSLOT)
        for p0 in range(0, nt, NSLOT):
            tiles = list(range(p0, min(nt, p0 + NSLOT)))
            gens = [self.gdn_pre_chain(ctx, gb, gb["slots"][i - p0], i, qkvT, ba) for i in tiles]
            live = list(gens)
            while live:
                for g_ in list(live):
                    try:
                        next(g_)
                    except StopIteration:
                        live.remove(g_)
            for i in tiles:
                self.gdn_post(ctx, gb, gb["slots"][i - p0], i)

    def gdn_bufs(self, es, nslot):
        sb = self.sb
        g = {}
        g["sq"] = sb("sq", [P, 8, P], BF16, es)
        g["tmp8"] = sb("tmp8", [P, 8, P], F32, es)
        g["knT"] = sb("knT", [P, NH, P], BF16, es)
        g["nda"] = sb("nda", [P, NH, P], F32, es)
        g["ndb"] = sb("ndb", [P, NH, P], F32, es)
        g["wT"] = sb("wT", [P, NH, P], BF16, es)
        g["vnew"] = sb("vnew", [P, NH, P], BF16, es)
        g["ob"] = sb("ob", [P, NH, P], BF16, es)
        g["slots"] = []
        for s in range(nslot):
            d = {}
            d["qnT"] = sb("qnT", [P, NH, P], BF16, es)
            d["gs"] = sb("gs", [P, 48], F32, es)
            d["Pk"] = [sb("Pk", [P, NH, P], F32, es) for _ in range(2)]
            d["Qk"] = [sb("Qk", [P, NH, P], F32, es) for _ in range(2)]
            d["QKm"] = sb("QKm", [P, NH, P], BF16, es)
            d["X"] = sb("X", [P, NH, 256], F32, es)
            d["kd"] = sb("kd", [P, NH, P], BF16, es)
            d["egl"] = sb("egl", [P, NH], F32, es)
            g["slots"].append(d)
        return g

    def gdn_pre_chain(self, ctx, gb, sl, i, qkvT, ba):
        seq = ctx["seq"]
        n = seq["n"]
        off = i * P
        L = 7 if n == P else int(math.ceil(math.log2(n)))
        sq, tmp8, knT, nda, ndb = (gb[k] for k in ("sq", "tmp8", "knT", "nda", "ndb"))
        rsq = tmp8
        dg = tmp8[:, 0:4]
        x1 = tmp8[:, 4:8]
        qnT, gs, Pk, Qk, QKm, X, kd, egl = (sl[k] for k in ("qnT", "gs", "Pk", "Qk", "QKm", "X", "kd", "egl"))
        qk3 = qkvT[:, 0:8, off:off + n]
        self.tt(sq[:, :, 0:n], qk3, qk3, ALU.mult)
        for half in range(2):
            bk = self.bank()
            b3 = bk[:].rearrange("p (h f) -> p h f", h=NH)
            self.mm(b3[:, :, 0:n], self.ones_b[:, :], sq[:, 4 * half:4 * half + 4, 0:n])
            self.act(rsq[:, 4 * half:4 * half + 4, 0:n], b3[:, :, 0:n], AF.Sqrt, bias=self.epsc[:, 0:1])
        self.recip(rsq[:, :, 0:n], rsq[:, :, 0:n])
        self.tt(qnT[:, :, 0:n], qkvT[:, 0:4, off:off + n], rsq[:, 0:4, 0:n], ALU.mult)
        self.tt(knT[:, :, 0:n], qkvT[:, 4:8, off:off + n], rsq[:, 4:8, 0:n], ALU.mult)
        self.tt(gs[:n, 0:4], ba[:n, i, 4:8], self.lp[:n, L_DT:L_DT + 4], ALU.add)
        self.act(gs[:n, 4:8], gs[:n, 0:4], AF.Abs)
        self.act(gs[:n, 4:8], gs[:n, 4:8], AF.Exp, scale=-1.0)
        self.act(gs[:n, 8:12], gs[:n, 4:8], AF.Ln, bias=self.onec[:n, 0:1])
        self.stt(gs[:n, 8:12], gs[:n, 0:4], 0.0, gs[:n, 8:12], ALU.max, ALU.add)
        self.tt(gs[:n, 12:16], gs[:n, 8:12], self.nA[:n, :], ALU.mult)
        self.act(gs[:n, 16:20], ba[:n, i, 0:4], AF.Sigmoid)
        bk = self.bank()
        self.mm(bk[:n, 0:4], self.U_f[:n, :n], gs[:n, 12:16])
        self.cp(gs[:n, 20:24], bk[:n, 0:4])
        self.act(gs[:n, 24:28], gs[:n, 20:24], AF.Exp)
        for h in range(NH):
            self.ts(dg[:n, h, 0:n], self.ident_f[:n, :n], gs[:n, 20 + h:21 + h], ALU.mult)
        rb = self.bank()
        rb3 = rb[:].rearrange("p (h f) -> p h f", h=NH)
        self.mm(rb3[:, :, 0:n], self.ones_f[:n, :], dg[:n, :, 0:n])
        for h in range(NH):
            self.ts(x1[:n, h, 0:n], rb3[:n, h, 0:n], gs[:n, 20 + h:21 + h], ALU.subtract)
        self.ts(nda[:n, :, 0:n], x1[:n, :, 0:n], 0.0, ALU.max)
        self.tt(ndb[:n, :, 0:n], nda[:n, :, 0:n], x1[:n, :, 0:n], ALU.subtract)
        self.act(nda[:n, :, 0:n], nda[:n, :, 0:n], AF.Exp, scale=-1.0)
        self.act(ndb[:n, :, 0:n], ndb[:n, :, 0:n], AF.Exp, scale=-1.0)
        self.tt(nda[:n, :, 0:n], nda[:n, :, 0:n], self.Mneg[:n, 0:n].unsqueeze(1).to_broadcast([n, NH, n]), ALU.mult)
        self.tt(ndb[:n, :, 0:n], ndb[:n, :, 0:n], self.Mup[:n, 0:n].unsqueeze(1).to_broadcast([n, NH, n]), ALU.mult)
        self.act(egl[:, :], rb3[:, :, n - 1], AF.Exp)
        self.tt(gs[:n, 32:36], rb3[:n, :, n - 1], gs[:n, 20:24], ALU.subtract)
        self.act(gs[:n, 32:36], gs[:n, 32:36], AF.Exp)
        self.tt(gs[:n, 28:32], gs[:n, 16:20], gs[:n, 24:28], ALU.mult)
        self.ts(gs[:n, 36:40], gs[:n, 24:28], SC, ALU.mult)
        bk = self.bank()
        b3 = bk[:].rearrange("p (h f) -> p h f", h=NH)
        for h in range(NH):
            self.mm(b3[:n, h, 0:n], knT[:, h, 0:n], knT[:, h, 0:n])
        for h in range(NH):
            self.stt(Pk[0][:n, h, 0:n], b3[:n, h, 0:n], gs[:n, 16 + h:17 + h], nda[:n, h, 0:n], ALU.mult, ALU.mult)
        bk = self.bank()
        bb = bk[:].rearrange("p (a b) -> p a b", a=NH)
        for h in range(NH):
            self.tr(bb[:n, h, 0:n], Pk[0][:n, h, 0:n], self.ident_f[:n, :n])
        self.cp(Qk[0][:n, :, 0:n], bb[:n, 0:NH, 0:n], eng=self.fw.act)
        bk = self.bank()
        b3 = bk[:].rearrange("p (h f) -> p h f", h=NH)
        for h in range(NH):
            self.mm(b3[:n, h, 0:n], knT[:, h, 0:n], qnT[:, h, 0:n])
        self.tt(QKm[:n, :, 0:n], b3[:n, :, 0:n], ndb[:n, :, 0:n], ALU.mult)
        bk = self.bank()
        kb_ = bk[:].bitcast(BF16).rearrange("p (a b) -> p a b", a=8)
        for h in range(NH):
            self.tr(kb_[:n, h, :], knT[:, h, 0:n], self.ident_b[:, :])
        for h in range(NH):
            self.tr(kb_[:n, 4 + h, :], qkvT[:, 8 + h, off:off + n], self.ident_b[:, :])
        self.tt(X[:n, :, 128:256], kb_[:n, 0:4, :], gs[:n, 28:32].unsqueeze(2).to_broadcast([n, NH, P]), ALU.mult)
        self.tt(kd[:n, :, :], kb_[:n, 0:4, :], gs[:n, 32:36].unsqueeze(2).to_broadcast([n, NH, P]), ALU.mult)
        self.tt(X[:n, :, 0:128], kb_[:n, 4:8, :], gs[:n, 16:20].unsqueeze(2).to_broadcast([n, NH, P]), ALU.mult)
        yield
        for k in range(L):
            Pc, Qc = Pk[k % 2], Qk[k % 2]
            Pn, Qn = Pk[(k + 1) % 2], Qk[(k + 1) % 2]
            if k < L - 1:
                pb = self.bank()
                qb = self.bank()
                p3 = pb[:].rearrange("p (h f) -> p h f", h=NH)
                q3 = qb[:].rearrange("p (h f) -> p h f", h=NH)
                for h in range(NH):
                    self.mm(p3[:n, h, 0:n], Qc[:n, h, 0:n], Pc[:n, h, 0:n])
                for h in range(NH):
                    self.mm(q3[:n, h, 0:n], Pc[:n, h, 0:n], Qc[:n, h, 0:n])
            y0 = self.bank()
            y1 = self.bank()
            ys = [y0[:].rearrange("p (h f) -> p h f", h=2), y1[:].rearrange("p (h f) -> p h f", h=2)]
            for h in range(NH):
                self.mm(ys[h // 2][:n, h % 2, :], Qc[:n, h, 0:n], X[:n, h, :])
            if k < L - 1:
                self.cp(Pn[:n, :, 0:n], p3[:n, :, 0:n], eng=self.fw.act)
                self.cp(Qn[:n, :, 0:n], q3[:n, :, 0:n], eng=self.fw.act)
            for hh in range(2):
                self.tt(X[:n, 2 * hh:2 * hh + 2, :], X[:n, 2 * hh:2 * hh + 2, :], ys[hh][:n, :, :], ALU.add)
            yield

    def gdn_post(self, ctx, gb, sl, i):
        seq = ctx["seq"]
        n = seq["n"]
        off = i * P
        tmp8, wT, vnew, ob = (gb[k] for k in ("tmp8", "wT", "vnew", "ob"))
        o1s = tmp8[:, 0:4]
        o = tmp8[:, 4:8]
        qnT, gs, QKm, X, kd, egl = (sl[k] for k in ("qnT", "gs", "QKm", "X", "kd", "egl"))
        bk = self.bank()
        bb = bk[:].rearrange("p (a b) -> p a b", a=NH)
        for h in range(NH):
            self.tr(bb[:, h, 0:n], X[:n, h, 128:256], self.ident_f[:n, :n])
        self.cp(wT[:, :, 0:n], bb[:, 0:NH, 0:n], eng=self.fw.act)
        bk = self.bank()
        b3 = bk[:].rearrange("p (h f) -> p h f", h=NH)
        for h in range(NH):
            self.mm(b3[:n, h, :], wT[:, h, 0:n], self.Sb_g[:, h, :])
        self.tt(vnew[:n, :, :], X[:n, :, 0:128], b3[:n, :, :], ALU.subtract)
        o1 = self.bank()
        o13 = o1[:].rearrange("p (h f) -> p h f", h=NH)
        for h in range(NH):
            self.mm(o13[:n, h, :], qnT[:, h, 0:n], self.Sb_g[:, h, :])
        o2 = self.bank()
        o23 = o2[:].rearrange("p (h f) -> p h f", h=NH)
        for h in range(NH):
            self.mm(o23[:n, h, :], QKm[:n, h, 0:n], vnew[:n, h, :])
        sn = self.bank()
        sn3 = sn[:].rearrange("p (h f) -> p h f", h=NH)
        for h in range(NH):
            self.mm(sn3[:, h, :], kd[:n, h, :], vnew[:n, h, :])
        for h in range(NH):
            self.stt(self.S_g[:, h, :], self.S_g[:, h, :], egl[:, h:h + 1], sn3[:, h, :], ALU.mult, ALU.add)
        self.cp(self.Sb_g[:], self.S_g[:], eng=self.fw.act)
        for h in range(NH):
            self.act(o1s[:n, h, :], o13[:n, h, :], AF.Copy, scale=gs[:n, 36 + h:37 + h])
        self.tt(o[:n, :, :], o1s[:n, :, :], o23[:n, :, :], ALU.add)
        for h in range(NH):
            self.act(o1s[:n, h, :], o[:n, h, :], AF.Square, accum_out=gs[:n, 40 + h:41 + h])
        self.rsqrt_(gs[:n, 44:48], gs[:n, 40:44], 1.0 / HD, EPS)
        self.tt(ob[:n, :, :], o[:n, :, :], gs[:n, 44:48].unsqueeze(2).to_broadcast([n, NH, P]), ALU.mult)
        self.finish_heads(ob, n, off, 0, gain=self.lp[:, L_GDN:L_GDN + 1])

    def interleave(self, gens):
        live = list(gens)
        while live:
            for g_ in list(live):
                try:
                    next(g_)
                except StopIteration:
                    live.remove(g_)

    def head_rms_T_gen(self, slot, i, n, gain, dst):
        gs = self.sm[:, 16 * i:16 * i + 16]
        junk = self.junk4[:, i, :]
        qb2 = self.qb24[:, i]
        bk = self.bank()
        self.proj_tm(slot, 256, i * P, n, bk[:n, 0:256])
        yield
        b3 = bk[:n, 0:256].rearrange("p (h e) -> p h e", h=2)
        for hh in range(2):
            self.act(junk[:n, 0:P], b3[:, hh, :], AF.Square, accum_out=gs[:n, hh:hh + 1])
        yield
        self.ts(gs[:n, 4:6], gs[:n, 0:2], 1.0 / HD, ALU.mult, EPS, ALU.add)
        yield
        self.act(gs[:n, 4:6], gs[:n, 4:6], AF.Sqrt)
        yield
        self.recip(gs[:n, 4:6], gs[:n, 4:6])
        yield
        self.tt(qb2[:n, :, :], b3, gs[:n, 4:6].unsqueeze(2).to_broadcast([n, 2, P]), ALU.mult)
        yield
        tb = self.bank()
        tbb = tb[:].bitcast(BF16).rearrange("p (a b) -> p a b", a=8)
        for hh in range(2):
            self.tr(tbb[:, hh, 0:n], qb2[:n, hh, :], self.ident_b[:n, :n])
        yield
        self.act(dst, tbb[:, 0:2, 0:n], AF.Identity, scale=gain)

    def group_b(self, ctx, es):
        self.mark("projB")
        seq, nt, ntok, tile0 = ctx["seq"], ctx["nt"], ctx["ntok"], ctx["tile0"]
        l, n = seq["l"], seq["n"]
        sb = self.sb
        w = self.w_in[l]
        qrsT = sb("qrsT", [P, 4, NH, P], BF16, es)
        krT = sb("krT", [P, 4, NH, P], BF16, es)
        krs = sb("krs", [P, 4, 512], BF16, es)
        vb = sb("vb", [P, 4, 512], BF16, es)
        stg = [sb("stg", [P, 256], F32, es) for _ in range(4)]
        rt = [sb("rt", [P, 4, 2, 64], F32, es) for _ in range(4)]
        rr = [sb("rr", [P, 2, P], F32, es) for _ in range(4)]
        rrb = [sb("rrb", [P, 2, P], BF16, es) for _ in range(4)]
        ropes = sb("ropes", [P, 4, P], F32, es)
        cd = self.cst[:, C_CD:C_CD + NH]
        sd = self.cst[:, C_SD128:C_SD128 + NH] if n == P else self.cst[:, C_SD32:C_SD32 + NH]
        for i in range(nt):
            r0 = seq["rope0"] + (tile0 + i) * P
            self.dma(ropes[:n, i, :], self.rope_d[r0:r0 + n, :])
        blocks = [(QB + 256 * i, 256) for i in range(8)]
        k_ = 0
        for b, slot in self.wblocks(w, blocks):
            if b < 6:
                h0 = 2 * (b % 2)

                def bunit(i, slot=slot, h0=h0, b=b):
                    bk = self.bank()
                    self.proj_tm(slot, 256, i * P, n, bk[:n, 0:256])
                    yield
                    if b >= 4:
                        self.act(vb[:n, i, h0 * P:(h0 + 2) * P], bk[:n, 0:256], AF.Copy)
                        return
                    s = stg[i]
                    rt_, rr_, rrb_ = rt[i], rr[i], rrb[i]
                    self.act(s[:n, :], bk[:n, 0:256], AF.Copy)
                    yield
                    s4 = s[:n, :].rearrange("p (h t d) -> p h t d", h=2, t=2)
                    x1, x2 = s4[:, :, 0, :], s4[:, :, 1, :]
                    cosb = ropes[:n, i, 0:64].unsqueeze(1).to_broadcast([n, 2, 64])
                    sinb = ropes[:n, i, 64:128].unsqueeze(1).to_broadcast([n, 2, 64])
                    self.tt(rt_[:n, 0], x1, cosb, ALU.mult)
                    yield
                    self.tt(rt_[:n, 1], x2, sinb, ALU.mult)
                    yield
                    self.tt(rt_[:n, 2], x1, sinb, ALU.mult)
                    yield
                    self.tt(rt_[:n, 3], x2, cosb, ALU.mult)
                    yield
                    rr4 = rr_[:n, :, :].rearrange("p h (t d) -> p h t d", t=2)
                    self.tt(rr4[:, :, 0, :], rt_[:n, 0], rt_[:n, 1], ALU.subtract)
                    yield
                    self.tt(rr4[:, :, 1, :], rt_[:n, 2], rt_[:n, 3], ALU.add)
                    yield
                    if b < 2:
                        self.tt(rrb_[:n, :, :], rr_[:n, :, :], cd[:n, h0:h0 + 2].unsqueeze(2).to_broadcast([n, 2, P]), ALU.mult)
                        dstT = qrsT
                    else:
                        self.cp(rrb_[:n, :, :], rr_[:n, :, :])
                        self.tt(krs[:n, i, h0 * P:(h0 + 2) * P].rearrange("p (h e) -> p h e", h=2), rr_[:n, :, :],
                                sd[:n, h0:h0 + 2].unsqueeze(2).to_broadcast([n, 2, P]), ALU.mult)
                        dstT = krT
                    yield
                    tb = self.bank()
                    tbb = tb[:].bitcast(BF16).rearrange("p (a b) -> p a b", a=8)
                    for hh in range(2):
                        self.tr(tbb[:, hh, 0:n], rrb_[:n, hh, :], self.ident_b[:n, :n])
                    yield
                    self.cp(dstT[:, i, h0:h0 + 2, 0:n], tbb[:, 0:2, 0:n], eng=self.fw.act)

                self.interleave([bunit(i) for i in range(nt)])
            else:
                for cc in range(2):
                    bk = self.bank()
                    self.proj_fm(slot, cc * P, P, ntok, bk[:, 0:ntok])
                    self.act(self.zT[:, 2 * (b - 6) + cc, 0:ntok], bk[:, 0:ntok], AF.Silu)
        self.mark("ret")
        MT = sb("MT", [P, NH, P], BF16, es)
        osb = sb("osb", [P, NH, P], F32, es)
        sqt = sb("sqt", [P, NH, P], F32, es)
        ob = sb("obr", [P, NH, P], BF16, es)
        gs = self.sm
        for i in range(nt):
            off = i * P
            sc = self.bank()
            sc3 = sc[:].rearrange("p (h f) -> p h f", h=NH)
            for h in range(NH):
                self.mm(sc3[:n, h, 0:n], krT[:, i, h, 0:n], qrsT[:, i, h, 0:n])
            self.tt(MT[:n, :, 0:n], sc3[:n, :, 0:n], self.retDm[:n, :, 0:n], ALU.mult)
            ob_ = self.bank()
            o3 = ob_[:].rearrange("p (h f) -> p h f", h=NH)
            for h in range(NH):
                self.mm(o3[:n, h, :], MT[:n, h, 0:n], vb[:n, i, h * P:(h + 1) * P], start=True, stop=False)
                self.mm(o3[:n, h, :], qrsT[:, i, h, 0:n], self.Sb_r[:, h, :], start=False, stop=True)
            self.act(osb[:n, :, :], o3[:n, :, :], AF.Copy)
            self.red(gs[:n, 0:4], osb[:n, :, :], ALU.add)
            self.tt(sqt[:n, :, :], osb[:n, :, :], osb[:n, :, :], ALU.mult)
            self.red(gs[:n, 4:8], sqt[:n, :, :], ALU.add)
            self.ts(gs[:n, 8:12], gs[:n, 0:4], 1.0 / HD, ALU.mult)
            self.tt(gs[:n, 12:16], gs[:n, 8:12], gs[:n, 8:12], ALU.mult)
            self.stt(gs[:n, 16:20], gs[:n, 4:8], 1.0 / HD, gs[:n, 12:16], ALU.mult, ALU.subtract)
            self.rsqrt_(gs[:n, 20:24], gs[:n, 16:20], 1.0, EPS)
            self.tt(sqt[:n, :, :], osb[:n, :, :], gs[:n, 8:12].unsqueeze(2).to_broadcast([n, NH, P]), ALU.subtract)
            self.tt(ob[:n, :, :], sqt[:n, :, :], gs[:n, 20:24].unsqueeze(2).to_broadcast([n, NH, P]), ALU.mult)
            self.finish_heads(ob, n, off, 4, gain=self.lp[:, L_RG:L_RG + 1], bias=self.lp[:, L_RB:L_RB + 1])
            sn = self.bank()
            sn3 = sn[:].rearrange("p (h f) -> p h f", h=NH)
            for h in range(NH):
                self.mm(sn3[:, h, :], krs[:n, i, h * P:(h + 1) * P], vb[:n, i, h * P:(h + 1) * P])
            for h in range(NH):
                gam = (1.0 - 2.0 ** (-5.0 - h)) ** n
                self.stt(self.S_r[:, h, :], self.S_r[:, h, :], float(gam), sn3[:, h, :], ALU.mult, ALU.add)
            self.cp(self.Sb_r[:], self.S_r[:], eng=self.fw.act)

    def group_d(self, ctx, es):
        self.mark("projD")
        seq, nt, ntok = ctx["seq"], ctx["nt"], ctx["ntok"]
        l, n = seq["l"], seq["n"]
        sb = self.sb
        w = self.w_in[l]
        QdT = sb("QdT", [P, 4, NH, P], BF16, es)
        E2 = sb("E2", [P, 2, NH, P], BF16, es)
        ob = sb("obd", [P, NH, P], BF16, es)
        gs = self.sm
        blocks = [(QD, 256), (QD + 256, 256), (ZD, 256), (ZD + 256, 256)]
        for b, slot in self.wblocks(w, blocks):
            if b < 2:
                self.interleave([self.head_rms_T_gen(slot, i, n, self.lp[:, L_MQ:L_MQ + 1], QdT[:, i, 2 * b:2 * b + 2, 0:n])
                                 for i in range(nt)])
            else:
                for cc in range(2):
                    bk = self.bank()
                    self.proj_fm(slot, cc * P, P, ntok, bk[:, 0:ntok])
                    self.act(self.zT[:, 2 * (b - 2) + cc, 0:ntok], bk[:, 0:ntok], AF.Silu)
        self.mark("mem")
        for i in range(nt):
            off = i * P
            for mb in range(2):
                lg = self.bank()
                lg3 = lg[:].rearrange("p (h f) -> p h f", h=NH)
                for h in range(NH):
                    self.mm(lg3[:, h, 0:n], self.mkT[:, h, mb * P:(mb + 1) * P], QdT[:, i, h, 0:n])
                self.act(E2[:, mb, :, 0:n], lg3[:, :, 0:n], AF.Exp, scale=SC)
            for h in range(NH):
                acc = self.bank()
                for mb in range(2):
                    self.mm(acc[:n, 0:HD + 1], E2[:, mb, h, 0:n], self.mvx[:, mb, h, :], start=(mb == 0), stop=(mb == 1))
                self.recip(gs[:n, h:h + 1], acc[:n, HD:HD + 1])
                self.ts(ob[:n, h, :], acc[:n, 0:HD], gs[:n, h:h + 1], ALU.mult)
            self.finish_heads(ob, n, off, 12)

    def memory_kv(self, l):
        self.mark("memkv")
        sb = self.sb
        gs = self.sm
        self.fw.barrier()
        with ExitStack() as es:
            kf = [sb("mkf", [P, 2, P], F32, es) for _ in range(2)]
            kb16 = sb("mkb", [P, 2, P], BF16, es)
            self.xt = [sb("xt", [P, D], F32, es) for _ in range(2)]
            self.xnb = [sb("xnb", [P, D], BF16, es) for _ in range(2)]
            self.smx = [sb("smx", [P, 2], F32, es) for _ in range(2)]
            for mt in range(2):
                self.norm_rows_to_T(self.mem_p[mt * P:(mt + 1) * P, :], P, L_MG, self.xnT, mt * P, bi=mt % 2)
            blocks = [(256 * b, 256) for b in range(4)]
            k_ = 0
            for b, slot in self.wblocks(self.w_mkv[l], blocks):
                for mt in range(2):
                    bk = self.bank()
                    self.proj_tm(slot, 256, mt * P, P, bk[:, 0:256])
                    b3 = bk[:, 0:256].rearrange("p (h e) -> p h e", h=2)
                    f = kf[k_ % 2]
                    k_ += 1
                    if b < 2:
                        for hh in range(2):
                            self.act(self.junk[:, 0:P], b3[:, hh, :], AF.Square, accum_out=gs[:, hh:hh + 1])
                        self.rsqrt_(gs[:, 4:6], gs[:, 0:2], 1.0 / HD, EPS)
                        self.tt(f[:, :, :], b3, gs[:, 4:6].unsqueeze(2).to_broadcast([P, 2, P]), ALU.mult)
                        self.tt(f[:, :, :], f[:, :, :], self.lp[:, L_MK:L_MK + P].unsqueeze(1).to_broadcast([P, 2, P]), ALU.mult)
                        self.dma(self.o_mk[l, mt * P:(mt + 1) * P, b * 256:(b + 1) * 256].rearrange("p (h e) -> p h e", h=2), f[:, :, :])
                        self.cp(kb16[:, :, :], f[:, :, :])
                        tb = self.bank()
                        tbb = tb[:].bitcast(BF16).rearrange("p (a b) -> p a b", a=8)
                        for hh in range(2):
                            self.tr(tbb[:, hh, :], kb16[:, hh, :], self.ident_b[:, :])
                        self.cp(self.mkT[:, 2 * b:2 * b + 2, mt * P:(mt + 1) * P], tbb[:, 0:2, :], eng=self.fw.act)
                    else:
                        self.act(f[:, :, :], b3, AF.Copy)
                        self.dma(self.o_mv[l, mt * P:(mt + 1) * P, (b - 2) * 256:(b - 1) * 256].rearrange("p (h e) -> p h e", h=2), f[:, :, :])
                        self.cp(self.mvx[:, mt, 2 * (b - 2):2 * (b - 2) + 2, 0:HD], f[:, :, :])
            self.fw.barrier()

    def sample_caches(self, l):
        self.mark("scache")
        sb = self.sb
        PAST = self.PAST
        self.fw.barrier()
        with ExitStack() as es:
            stg = [sb("cstg", [P, 512], F32, es) for _ in range(2)]
            sb16 = [sb("csb", [P, 512], BF16, es) for _ in range(2)]
            ktt = [sb("cktt", [P, NH, P], BF16, es) for _ in range(2)]
            kis = [sb("ckis", [P, 64], F32, es) for _ in range(2)]
            ki2 = [sb("cki2", [P, P], BF16, es) for _ in range(2)]
            for mt in range(2):
                s, s16 = stg[mt % 2], sb16[mt % 2]
                self.dma(s[:, :], self.cmk_s[l, mt * P:(mt + 1) * P, :])
                self.cp(s16[:, :], s[:, :])
                tb = self.bank()
                tbb = tb[:].bitcast(BF16).rearrange("p (a b) -> p a b", a=8)
                for h in range(NH):
                    self.tr(tbb[:, h, :], s16[:, h * P:(h + 1) * P], self.ident_b[:, :])
                self.cp(self.mkT[:, :, mt * P:(mt + 1) * P], tbb[:, 0:NH, :], eng=self.fw.act)
                self.dma(self.mvx[:, mt, :, 0:HD], self.cmv_s[l, mt * P:(mt + 1) * P, :].rearrange("p (h e) -> p h e", h=NH), q=self.fw.pool)
            for r0 in range(0, PAST, 1024):
                r1 = min(PAST, r0 + 1024)
                self.dma(self.vbs[r0:r1, :], self.cv_s[l, r0:r1, :], q=self.fw.pool, max_dma_last_dim=2048)
            for kb in range(PAST // P):
                s, s16, kt = stg[kb % 2], sb16[kb % 2], ktt[kb % 2]
                self.dma(s[:, :], self.ck_s[l, kb * P:(kb + 1) * P, :])
                self.cp(s16[:, :], s[:, :])
                tb = self.bank()
                tbb = tb[:].bitcast(BF16).rearrange("p (a b) -> p a b", a=8)
                for h in range(NH):
                    self.tr(tbb[:, h, :], s16[:, h * P:(h + 1) * P], self.ident_b[:, :])
                self.cp(kt[:, :, :], tbb[:, 0:NH, :], eng=self.fw.act)
                self.dma(self.kts[:, kb // 2, :, (kb % 2) * P:(kb % 2) * P + P], kt[:, :, :])
                ks, k2 = kis[kb % 2], ki2[kb % 2]
                self.dma(ks[:, :], self.cik_s[l, kb * P:(kb + 1) * P, :])
                self.cp(k2[:, 0:64], ks[:, :])
                self.cp(k2[:, 64:128], ks[:, :])
                tb = self.bank()
                tbb = tb[:].bitcast(BF16)
                self.tr(tbb[:, 0:P], k2[:, :], self.ident_b[:, :])
                self.cp(self.kiT2[:, kb * P:(kb + 1) * P], tbb[:, 0:P], eng=self.fw.act)
            self.fw.barrier()

    def group_c(self, ctx, es):
        self.mark("projC")
        seq, nt, ntok, tile0 = ctx["seq"], ctx["nt"], ctx["ntok"], ctx["tile0"]
        l, n, kind, past = seq["l"], seq["n"], seq["kind"], seq["past"]
        sb = self.sb
        w = self.w_in[l]
        gs = self.sm
        QcT = sb("QcT", [P, 4, NH, P], BF16, es)
        qiZ = sb("qiZ", [P, 16, 512], BF16, es)
        qz4 = qiZ[:].rearrange("p (c two) t -> p c two t", two=2)
        self.mset(qz4[64:128, :, 0, :], 0.0)
        self.mset(qz4[0:64, :, 1, :], 0.0)
        absw = sb("absw", [P, 4, 16], F32, es)
        sgn = sb("sgn", [P, 4, 16], F32, es)
        kf = [sb("kf", [P, 2, P], F32, es) for _ in range(4)]
        kb16 = [sb("kb16", [P, 2, P], BF16, es) for _ in range(4)]
        ktt = [sb("ktt", [P, 2, P], BF16, es) for _ in range(4)]
        vf = [sb("vf", [P, 256], F32, es) for _ in range(2)]
        vb16 = [sb("vb16", [P, 256], BF16, es) for _ in range(2)]
        kif = [sb("kif", [P, 64], F32, es) for _ in range(2)]
        ki2 = sb("ki2", [P, P], BF16, es)
        index = sb("index", [P, self.SMAX], F32, es)
        cjd = sb("cjd", [P, int(0.42 * self.SMAX) + 8], mybir.dt.uint8, es)
        cja = sb("cja", [P, int(0.58 * self.SMAX) + 16], mybir.dt.uint8, es)
        dsg = sb("dsg", [P, 16, P], BF16, es)
        R = [sb("R", [P, 512], BF16, es) for _ in range(6)]
        Kx = [sb("Kx", [P, NH, 256], BF16, es) for _ in range(2)]
        Vx = [sb("Vx", [P, 2, NH, HD + 1], BF16, es) for _ in range(2)]
        PT = [sb("PT", [P, NH, P], BF16, es) for _ in range(3)]
        selb = [sb("selb", [P, P], BF16, es) for _ in range(3)]
        ob = sb("obc", [P, NH, P], BF16, es)
        bs = sb("bs", [P, 16], F32, es)
        bw = sb("bw", [P, 20], F32, es)
        bsa = sb("bsa", [P, 2], F32, es)
        bsb = sb("bsb", [P, 2], F32, es)
        bw2 = sb("bw2", [P, 20], F32, es)
        for v_ in Vx:
            self.mset(v_[:], 1.0)
        blocks = ([(QC + 256 * i, 256) for i in range(8)] + [(QI + 256 * i, 256) for i in range(4)] + [(KI, 80)])
        k_ = 0
        for b, slot in self.wblocks(w, blocks):
            if b < 2:
                self.interleave([self.head_rms_T_gen(slot, i, n, self.lp[:, L_DQ:L_DQ + 1], QcT[:, i, 2 * b:2 * b + 2, 0:n])
                                 for i in range(nt)])
            elif b < 4:
                h0 = 2 * (b - 2)

                def kunit(i, slot=slot, h0=h0):
                    g4 = self.sm[:, 16 * i:16 * i + 16]
                    junk = self.junk4[:, i, :]
                    bk = self.bank()
                    self.proj_tm(slot, 256, i * P, n, bk[:n, 0:256])
                    yield
                    b3 = bk[:n, 0:256].rearrange("p (h e) -> p h e", h=2)
                    f, kt, k16 = kf[i], ktt[i], kb16[i]
                    for hh in range(2):
                        self.act(junk[:n, 0:P], b3[:, hh, :], AF.Square, accum_out=g4[:n, hh:hh + 1])
                    yield
                    self.ts(g4[:n, 4:6], g4[:n, 0:2], 1.0 / HD, ALU.mult, EPS, ALU.add)
                    yield
                    self.act(g4[:n, 4:6], g4[:n, 4:6], AF.Sqrt)
                    yield
                    self.recip(g4[:n, 4:6], g4[:n, 4:6])
                    yield
                    self.tt(f[:n, :, :], b3, g4[:n, 4:6].unsqueeze(2).to_broadcast([n, 2, P]), ALU.mult)
                    yield
                    self.tt(f[:n, :, :], f[:n, :, :], self.lp[:n, L_DK:L_DK + P].unsqueeze(1).to_broadcast([n, 2, P]), ALU.mult)
                    yield
                    r0 = (tile0 + i) * P
                    self.dma(self.o_dk[kind][l, r0:r0 + n, h0 * P:(h0 + 2) * P].rearrange("p (h e) -> p h e", h=2), f[:n, :, :])
                    self.cp(k16[:n, :, :], f[:n, :, :])
                    yield
                    tb = self.bank()
                    tbb = tb[:].bitcast(BF16).rearrange("p (a b) -> p a b", a=8)
                    for hh in range(2):
                        self.tr(tbb[:, hh, 0:n], k16[:n, hh, :], self.ident_b[:n, :n])
                    yield
                    self.cp(kt[:, :, 0:n], tbb[:, 0:2, 0:n], eng=self.fw.act)
                    yield
                    kpos = past + r0
                    self.dma(self.kts[:, kpos // 256, h0:h0 + 2, kpos % 256:kpos % 256 + n], kt[:, :, 0:n])

                self.interleave([kunit(i) for i in range(nt)])
            elif b < 6:
                h0 = 2 * (b - 4)
                for i in range(nt):
                    bk = self.bank()
                    self.proj_tm(slot, 256, i * P, n, bk[:n, 0:256])
                    f = vf[k_ % 2]
                    k_ += 1
                    self.act(f[:n, :], bk[:n, 0:256], AF.Copy)
                    r0 = (tile0 + i) * P
                    self.dma(self.o_dv[kind][l, r0:r0 + n, h0 * P:(h0 + 2) * P], f[:n, :])
                    vb_ = vb16[k_ % 2]
                    self.cp(vb_[:n, :], f[:n, :])
                    self.dma(self.vbs[past + r0:past + r0 + n, h0 * P:(h0 + 2) * P], vb_[:n, :])
            elif b < 8:
                for cc in range(2):
                    bk = self.bank()
                    self.proj_fm(slot, cc * P, P, ntok, bk[:, 0:ntok])
                    self.act(self.zT[:, 2 * (b - 6) + cc, 0:ntok], bk[:, 0:ntok], AF.Silu)
            elif b < 12:
                for cc in range(2):
                    bk = self.bank()
                    self.proj_fm(slot, cc * P, P, ntok, bk[:, 0:ntok])
                    c_ = 2 * (b - 8) + cc
                    self.act(qz4[0:64, c_, 0, 0:ntok], bk[0:64, 0:ntok], AF.Copy)
                    self.act(qz4[64:128, c_, 1, 0:ntok], bk[64:128, 0:ntok], AF.Copy)
            else:
                for i in range(nt):
                    bk = self.bank()
                    self.proj_tm(slot, 80, i * P, n, bk[:n, 0:80])
                    f = kif[i % 2]
                    self.act(self.junk[:n, 0:64], bk[:n, 0:64], AF.Square, accum_out=gs[:n, 0:1])
                    self.rsqrt_(gs[:n, 4:5], gs[:n, 0:1], 1.0 / 64, EPS)
                    self.ts(f[:n, :], bk[:n, 0:64], gs[:n, 4:5], ALU.mult)
                    self.tt(f[:n, :], f[:n, :], self.lp[:n, L_IK:L_IK + 64], ALU.mult)
                    r0 = (tile0 + i) * P
                    self.dma(self.o_ik[kind][l, r0:r0 + n, :], f[:n, :])
                    self.cp(ki2[:n, 0:64], f[:n, :])
                    self.cp(ki2[:n, 64:128], f[:n, :])
                    tb = self.bank()
                    tbb = tb[:].bitcast(BF16)
                    self.tr(tbb[:, 0:n], ki2[:n, :], self.ident_b[:n, :n])
                    kpos = past + r0
                    self.cp(self.kiT2[:, kpos:kpos + n], tbb[:, 0:n], eng=self.fw.act)
                    self.act(absw[:n, i, :], bk[:n, 64:80], AF.Abs)
                    self.ts(sgn[:n, i, :], bk[:n, 64:80], 0.0, ALU.is_ge, 2.0, ALU.mult)
                    self.ts(sgn[:n, i, :], sgn[:n, i, :], -1.0, ALU.add)
        NIT = 16
        for i in range(nt):
            off = i * P
            j = tile0 + i
            S = past + (j + 1) * n if kind == 0 else past + n
            self.mark("indexer")
            self.tt(dsg[:n, :, 0:n], self.ident_b[:n, 0:n].unsqueeze(1).to_broadcast([n, 16, n]),
                    sgn[:n, i, :].unsqueeze(2).to_broadcast([n, 16, n]), ALU.mult)
            nch = (S + 511) // 512
            for c in range(nch):
                wd = min(512, S - 512 * c)
                ibid = self.bank_hold(1)
                ib = self.banks[ibid[0]]
                pend = []
                for h in range(16):
                    base = 64 * (h % 2)
                    ch = h // 2
                    sc = self.bank()
                    self.mm(sc[:n, 0:wd], qiZ[:, h, off:off + n], self.kiT2[:, 512 * c:512 * c + wd])
                    r = R[h % 6]
                    if h % 2 == 0:
                        self.act(r[:n, 0:wd], sc[:n, 0:wd], AF.Relu, scale=absw[:n, i, h:h + 1])
                    else:
                        self.ts(r[:n, 0:wd], sc[:n, 0:wd], absw[:n, i, h:h + 1], ALU.mult, 0.0, ALU.max)
                    pend.append((h, r))
                    if len(pend) > 2:
                        ph, pr = pend.pop(0)
                        self.mm(ib[:n, 0:wd], dsg[:n, ph, 0:n], pr[:n, 0:wd], start=(ph == 0), stop=False)
                while pend:
                    ph, pr = pend.pop(0)
                    self.mm(ib[:n, 0:wd], dsg[:n, ph, 0:n], pr[:n, 0:wd], start=(ph == 0), stop=(ph == 15))
                self.cp(index[:n, 512 * c:512 * c + wd], ib[:n, 0:wd], eng=(self.fw.act if c % 2 == 0 else self.fw.dve))
                self.bank_release(ibid)
            self.mark("bisect")
            mid, _u1, c1, _u2, v_, t_, mn_, mx_, thr0 = (bs[:n, k:k + 1] for k in range(9))
            nmid = bsa[:n, 0:1]
            sg = bsb[:n, 0:1]
            if S > TOPK:
                self.red(mx_, index[:n, 0:S], ALU.max)
                self.red(mn_, index[:n, 0:S], ALU.min)
                self.tt(t_, mx_, mn_, ALU.subtract)
                self.ts(t_, t_, 0.5, ALU.mult, 1.0, ALU.add)
                self.ts(bw[:n, 0:NIT + 1], self.p2[:n, 0:NIT + 1], t_, ALU.mult)
                self.ts(bw2[:n, 0:NIT + 1], bw[:n, 0:NIT + 1], 2.0, ALU.mult)
                self.tt(mid, mx_, mn_, ALU.add)
                self.ts(mid, mid, 0.5, ALU.mult)
                self.ts(nmid, mid, -1.0, ALU.mult)
            if kind == 0:
                self.mset(index[0:64, S - 64:S], NEG)
            if S > TOPK:
                Sd = (int(0.42 * S) // 8) * 8
                Sa = S - Sd
                for it in range(NIT):
                    self.ts(cjd[:n, 0:Sd], index[:n, 0:Sd], mid, ALU.is_gt, None, ALU.add, accum_out=c1)
                    self.act(cja[:n, 0:Sa], index[:n, Sd:S], AF.Sign, bias=nmid, accum_out=sg)
                    self.stt(v_, c1, 2.0, sg, ALU.mult, ALU.add)
                    self.stt(t_, v_, float(2 * TOPK - 1 - Sa), bw2[:n, it + 1:it + 2], ALU.is_ge, ALU.mult)
                    self.stt(mid, t_, bw[:n, it + 1:it + 2], mid, ALU.subtract, ALU.add)
                    self.stt(nmid, nmid, bw[:n, it + 1:it + 2], t_, ALU.add, ALU.subtract)
                thr = thr0
                self.tt(thr, mid, bw[:n, NIT:NIT + 1], ALU.subtract)
            else:
                thr = thr0
                self.mset(thr, -1.0e38)
            self.mark("attn")
            accs = self.bank_hold(4)
            nblk = (S + P - 1) // P
            npair = (nblk + 1) // 2

            def issue_load(kb2):
                slot = kb2 % 2
                k0 = kb2 * 256
                kw2 = min(256, S - k0)
                self.dma(Kx[slot][:, :, 0:kw2], self.kts[:, kb2, :, 0:kw2])
                for bq in range(2):
                    kb = 2 * kb2 + bq
                    if kb >= nblk:
                        break
                    kw = min(P, S - kb * P)
                    self.dma(Vx[slot][:kw, bq, :, 0:HD], self.vbs[kb * P:kb * P + kw, :].rearrange("s (h e) -> s h e", h=NH))

            def emit_pv(pv):
                kb, slot, bq, kw, PT_ = pv
                for h in range(NH):
                    self.mm(self.banks[accs[h]][:n, 0:HD + 1], PT_[:kw, h, 0:n], Vx[slot][:kw, bq, h, :],
                            start=(kb == 0), stop=(kb == nblk - 1))

            issue_load(0)
            pend = None
            for kb2 in range(npair):
                slot = kb2 % 2
                for bq in range(2):
                    kb = 2 * kb2 + bq
                    if kb >= nblk:
                        break
                    kw = min(P, S - kb * P)
                    sb_ = selb[kb % 3]
                    self.ts(sb_[:n, 0:kw], index[:n, kb * P:kb * P + kw], thr, ALU.is_le)
                    lg = self.bank()
                    lg3 = lg[:].rearrange("p (h f) -> p h f", h=NH)
                    for h in range(NH):
                        self.mm(lg3[:kw, h, 0:n], Kx[slot][:, h, bq * P:bq * P + kw], QcT[:, i, h, 0:n], start=(h == 0), stop=False)
                    self.mm(lg3[:kw, :, 0:n], sb_[:n, 0:kw], self.negI4[:n, :, 0:n], start=False, stop=True)
                    PT_ = PT[kb % 3]
                    self.act(PT_[:kw, :, 0:n], lg3[:kw, :, 0:n], AF.Exp, scale=SC)
                    if pend is not None:
                        emit_pv(pend)
                    if bq == 0 and kb2 + 1 < npair:
                        issue_load(kb2 + 1)
                    pend = (kb, slot, bq, kw, PT_)
            emit_pv(pend)
            for h in range(NH):
                a_ = self.banks[accs[h]]
                self.recip(gs[:n, h:h + 1], a_[:n, HD:HD + 1])
                self.ts(ob[:n, h, :], a_[:n, 0:HD], gs[:n, h:h + 1], ALU.mult)
            self.bank_release(accs)
            self.finish_heads(ob, n, off, 8)

    def out_proj(self, ctx):
        seq, nt, tile0 = ctx["seq"], ctx["nt"], ctx["tile0"]
        l, n = seq["l"], seq["n"]
        blocks = [(256 * b, 256) for b in range(8)]
        k_ = 0
        with ExitStack() as es:
            xfull = self.sb("xfull", [P, 4, D], F32, es)
            ycs = [self.sb("ycr", [P, 256], F32, es) for _ in range(6)]
            for i in range(nt):
                r0 = (tile0 + i) * P
                self.dma(xfull[:n, i, :], seq["xsrc"][r0:r0 + n, :], q=self.fw.act)
            for b, slot in self.wblocks(self.w_out[l], blocks):
                for i in range(nt):
                    r0 = (tile0 + i) * P
                    bk = self.bank()
                    self.proj_tm(slot, 256, i * P, n, bk[:n, 0:256], src=self.mixT)
                    yc = ycs[k_ % 6]
                    k_ += 1
                    self.tt(yc[:n, :], bk[:n, 0:256], xfull[:n, i, 256 * b:256 * b + 256], ALU.add)
                    self.dma(seq["ydst"][r0:r0 + n, 256 * b:256 * b + 256], yc[:n, :], q=self.fw.act)
            self.fw.barrier()


class nc_noncontig:
    def __init__(self, nc):
        self.cm = nc.allow_non_contiguous_dma(reason="small strided param/state transfers")

    def __enter__(self):
        return self.cm.__enter__()

    def __exit__(self, *a):
        return self.cm.__exit__(*a)


def make_consts():
    c = np.zeros((P, NCST), np.float32)
    p = np.arange(P)[:, None]
    f = np.arange(P)[None, :]
    c[:, C_ID:C_ID + P] = (p == f)
    c[:, C_ONE:C_ONE + P] = 1.0
    c[:, C_U:C_U + P] = (p <= f)
    c[:, C_MNEG:C_MNEG + P] = -1.0 * (p > f)
    c[:, C_MUP:C_MUP + P] = SC * (f >= p)
    for h in range(NH):
        gam = 1.0 - 2.0 ** (-5.0 - h)
        c[:, C_RDM + h * P:C_RDM + (h + 1) * P] = (gam ** (-(p + 1.0))) * SC * (f >= p)
        c[:, C_CD + h] = gam ** (np.arange(P) + 1.0)
        c[:, C_SD128 + h] = gam ** (127.0 - np.arange(P)) * SC
        c[:32, C_SD32 + h] = gam ** (31.0 - np.arange(32)) * SC
    return c


def make_rope(T_, TS, PAST):
    half = 64
    inv = (10000.0 ** (-np.arange(half, dtype=np.float32) / half)).astype(np.float32)
    pos = np.concatenate([np.arange(T_), PAST + np.arange(TS)]).astype(np.float32)
    ang = (pos[:, None] * inv[None, :]).astype(np.float32)
    return np.concatenate([np.cos(ang), np.sin(ang)], axis=1).astype(np.float32)


def make_lp(inp, DEPTH):
    lp = np.zeros((DEPTH, P, NLP), np.float32)
    for l in range(DEPTH):
        lp[l, :, L_G:L_G + 16] = inp["norm_g"][l].reshape(16, P).T
        lp[l, :, L_MG:L_MG + 16] = inp["mem_norm_g"][l].reshape(16, P).T
        cw = inp["gdn_conv_w"][l]
        lp[l, :, L_CW:L_CW + 48] = cw.reshape(4, 12, P).transpose(2, 1, 0).reshape(P, 48)
        lp[l, :, L_AL:L_AL + 4] = inp["gdn_a_log"][l][None, :]
        lp[l, :, L_DT:L_DT + 4] = inp["gdn_dt_bias"][l][None, :]
        lp[l, :, L_GDN] = inp["gdn_norm_g"][l]
        lp[l, :, L_RG] = inp["ret_norm_g"][l]
        lp[l, :, L_RB] = inp["ret_norm_b"][l]
        lp[l, :, L_DQ] = inp["dsa_q_norm_g"][l]
        lp[l, :, L_MQ] = inp["mem_q_norm_g"][l]
        lp[l, :, L_DK:L_DK + P] = inp["dsa_k_norm_g"][l][None, :]
        lp[l, :, L_MK:L_MK + P] = inp["mem_k_norm_g"][l][None, :]
        lp[l, :, L_IK:L_IK + 64] = inp["idx_k_norm_g"][l][None, :]
    return lp


def run(inp, T_, TS, PAST, DEPTH, dbg_cols=0, ncores=8):
    inp = {k: np.asarray(v) for k, v in inp.items()}
    bld = Builder(T_, TS, PAST, DEPTH, dbg_cols)
    nc = bld.build()
    cst = make_consts()
    rope = make_rope(T_, TS, PAST)
    lp = make_lp(inp, DEPTH)
    ca = np.ascontiguousarray
    w_in, w_mkv, w_out = ca(inp["w_in"]), ca(inp["w_mem_kv"]), ca(inp["w_out"])
    NB = inp["x_prompt"].shape[0]
    NSB = inp["x_sample"].shape[0]
    in_maps = []
    for c in range(ncores):
        pb = (c // 2) % NB
        sbi = c % NSB
        in_maps.append(dict(
            x_p=ca(inp["x_prompt"][pb]), x_s=ca(inp["x_sample"][sbi]),
            conv_s=ca(inp["cache_gdn_conv"][:, sbi]), sg_s=ca(inp["state_gdn"][:, sbi]), sr_s=ca(inp["state_ret"][:, sbi]),
            ck_s=ca(inp["cache_dsa_k"][:, sbi].reshape(DEPTH, PAST, 512)), cv_s=ca(inp["cache_dsa_v"][:, sbi].reshape(DEPTH, PAST, 512)),
            cik_s=ca(inp["cache_idx_k"][:, sbi]), cmk_s=ca(inp["cache_mem_k"][:, sbi].reshape(DEPTH, NMEM, 512)),
            cmv_s=ca(inp["cache_mem_v"][:, sbi].reshape(DEPTH, NMEM, 512)), mem_p=ca(inp["mem_prompt"][pb]),
            w_in=w_in, w_mkv=w_mkv, w_out=w_out, lp=lp, cst=cst, rope=rope))
    res = run_bass_kernel_spmd(nc, in_maps, core_ids=list(range(ncores)))
    R = res.results

    def stack_p(name, shp):
        return np.stack([np.asarray(R[2 * b][name]).reshape(shp) for b in range(NB)], axis=1)

    def stack_s(name, shp):
        return np.stack([np.asarray(R[c][name]).reshape(shp) for c in range(NSB)], axis=1)

    y_p = np.stack([np.asarray(R[2 * b]["y_p"]) for b in range(NB)], axis=0)
    y_s = np.stack([np.asarray(R[c]["y_s"]) for c in range(NSB)], axis=0)
    outs = (
        y_p, y_s,
        stack_p("o_conv_p", (DEPTH, 3, 1536)), stack_p("o_gdn_p", (DEPTH, NH, HD, HD)), stack_p("o_ret_p", (DEPTH, NH, HD, HD)),
        stack_p("o_dk_p", (DEPTH, T_, NH, HD)), stack_p("o_dv_p", (DEPTH, T_, NH, HD)), stack_p("o_ik_p", (DEPTH, T_, 64)),
        stack_p("o_mk_p", (DEPTH, NMEM, NH, HD)), stack_p("o_mv_p", (DEPTH, NMEM, NH, HD)),
        stack_s("o_conv_s", (DEPTH, 3, 1536)), stack_s("o_gdn_s", (DEPTH, NH, HD, HD)), stack_s("o_ret_s", (DEPTH, NH, HD, HD)),
        stack_s("o_dk_s", (DEPTH, TS, NH, HD)), stack_s("o_dv_s", (DEPTH, TS, NH, HD)), stack_s("o_ik_s", (DEPTH, TS, 64)),
    )
    outs = tuple(np.ascontiguousarray(o, dtype=np.float32) for o in outs)
    if dbg_cols:
        return outs, [np.asarray(R[c]["dbg"]) for c in range(ncores)]
    return outs


def kernel(**inputs):
    return run(inputs, 8192, 32, 4096, 2)
```
